# Optimizing a Trainium2 kernel written in Bass

```python
import math
import jax, jax.numpy as jnp
from jax import lax
import numpy as np

D_MODEL = 1024
BATCH = 4
SEQ = 8192
DEPTH = 2

EPS = 1e-6
SC_WIDTH = 1024
SC_CONV_WIDTH = 3
SSM_D_INNER = 1024
SSM_HEADDIM = 64
SSM_HEADS = SSM_D_INNER // SSM_HEADDIM
SSM_GROUPS = 4
SSM_HPG = SSM_HEADS // SSM_GROUPS
SSM_STATE = 128
SSM_CONV_WIDTH = 4
SSM_CHUNK = 128
SSM_CONV_DIM = SSM_D_INNER + 2 * SSM_GROUPS * SSM_STATE
GLA_HEADS = 4
GLA_KEY_DIM = D_MODEL // 2
GLA_VAL_DIM = D_MODEL
GLA_HEAD_K = GLA_KEY_DIM // GLA_HEADS
GLA_HEAD_V = GLA_VAL_DIM // GLA_HEADS
GLA_GATE_RANK = 16
GLA_GATE_NORMALIZER = 16.0
GLA_CHUNK = 64
N_BRANCHES = 3
D_FF = 4 * D_MODEL

SPLIT_SIZES = (
    SC_WIDTH, SC_WIDTH, SC_WIDTH,
    SSM_D_INNER, SSM_CONV_DIM, SSM_HEADS,
    GLA_KEY_DIM, GLA_KEY_DIM, GLA_VAL_DIM, GLA_VAL_DIM, GLA_GATE_RANK,
    N_BRANCHES * D_MODEL,
)
IN_WIDTH = 12320

kernel_name = "hybrid_conv_ssd_gla_block"


def split_cols(u, sizes):
    outs, start = [], 0
    for s in sizes:
        outs.append(u[..., start:start + s])
        start += s
    return outs


def rms_norm(x, w):
    xf = x.astype(jnp.float32)
    y = xf * lax.rsqrt(jnp.mean(xf * xf, axis=-1, keepdims=True) + EPS)
    return (y * w.astype(jnp.float32)).astype(x.dtype)


def causal_dwconv(u, w):
    K = w.shape[0]
    s = u.shape[1]
    up = jnp.pad(u, ((0, 0), (K - 1, 0), (0, 0)))
    y = up[:, 0:s, :] * w[0]
    for k in range(1, K):
        y = y + up[:, k:k + s, :] * w[k]
    return y


def short_conv_mixer(a_x, a_b, a_c, conv_w, w_out):
    u = causal_dwconv(a_c * a_x, conv_w)
    return (a_b * u) @ w_out


def mamba2_ssd(z, xbc, dt_raw, conv_w, conv_b, dt_bias, a_log, d_skip, norm_w, w_out):
    b, s, _ = z.shape
    f32 = jnp.float32
    G, R, P, N, Q = SSM_GROUPS, SSM_HPG, SSM_HEADDIM, SSM_STATE, SSM_CHUNK
    nc = s // Q
    xbc = jax.nn.silu(causal_dwconv(xbc, conv_w) + conv_b)
    xs, bm, cm = split_cols(xbc, (SSM_D_INNER, G * N, G * N))
    dt = jax.nn.softplus((dt_raw + dt_bias).astype(f32))
    a = -jnp.exp(a_log.astype(f32)).reshape(G, R)
    xh = xs.astype(f32).reshape(b, nc, Q, G, R, P)
    bm = bm.astype(f32).reshape(b, nc, Q, G, N)
    cm = cm.astype(f32).reshape(b, nc, Q, G, N)
    dt_c = dt.reshape(b, nc, Q, G, R)
    a_cum = jnp.cumsum(dt_c * a, axis=2)
    xdt = xh * dt_c[..., None]
    causal = jnp.tril(jnp.ones((Q, Q), dtype=bool))[None, None, :, :, None, None]
    seg = a_cum[:, :, :, None] - a_cum[:, :, None]
    decay_ls = jnp.where(causal, jnp.exp(jnp.where(causal, seg, 0.0)), 0.0)
    cb = jnp.einsum("bclgn,bcsgn->bclsg", cm, bm)
    y_diag = jnp.einsum("bclsg,bclsgr,bcsgrp->bclgrp", cb, decay_ls, xdt)
    decay_to_end = jnp.exp(a_cum[:, :, -1:] - a_cum)
    states = jnp.einsum("bclgn,bclgr,bclgrp->bcgrpn", bm, decay_to_end, xdt)
    chunk_decay = jnp.exp(a_cum[:, :, -1])

    def step(carry, inp):
        st, dec = inp
        return dec[..., None, None] * carry + st, carry

    init = jnp.zeros((b, G, R, P, N), f32)
    _, prev = lax.scan(step, init, (jnp.moveaxis(states, 1, 0), jnp.moveaxis(chunk_decay, 1, 0)))
    prev = jnp.moveaxis(prev, 0, 1)
    y_off = jnp.einsum("bclgn,bcgrpn,bclgr->bclgrp", cm, prev, jnp.exp(a_cum))
    y = y_diag + y_off + xh * d_skip.astype(f32).reshape(G, R)[:, :, None]
    y = y.reshape(b, s, SSM_D_INNER) * jax.nn.silu(z.astype(f32))
    yg = y.reshape(b, s, G, SSM_D_INNER // G)
    yg = yg * lax.rsqrt(jnp.mean(yg * yg, axis=-1, keepdims=True) + EPS)
    y = yg.reshape(b, s, SSM_D_INNER) * norm_w.astype(f32)
    return y.astype(z.dtype) @ w_out


def gla_mixer(q, k, v, g, gk_low, w_gk2, b_gk, norm_w, w_out):
    b, s, _ = q.shape
    f32 = jnp.float32
    H, DK, DV, T = GLA_HEADS, GLA_HEAD_K, GLA_HEAD_V, GLA_CHUNK
    nc = s // T
    gk = jax.nn.log_sigmoid((gk_low @ w_gk2 + b_gk).astype(f32)) / GLA_GATE_NORMALIZER
    qh = q.astype(f32).reshape(b, nc, T, H, DK) * (DK ** -0.5)
    kh = k.astype(f32).reshape(b, nc, T, H, DK)
    vh = v.astype(f32).reshape(b, nc, T, H, DV)
    gcum = jnp.cumsum(gk.reshape(b, nc, T, H, DK), axis=2)
    q_in = qh * jnp.exp(gcum)
    k_in = kh * jnp.exp(-gcum)
    causal = jnp.tril(jnp.ones((T, T), dtype=bool))[None, None, None]
    scores = jnp.where(causal, jnp.einsum("bclhd,bcshd->bchls", q_in, k_in), 0.0)
    o_intra = jnp.einsum("bchls,bcshv->bclhv", scores, vh)
    g_last = gcum[:, :, -1]
    k_end = kh * jnp.exp(g_last[:, :, None] - gcum)
    chunk_kv = jnp.einsum("bclhd,bclhv->bchdv", k_end, vh)
    chunk_decay = jnp.exp(g_last)

    def step(carry, inp):
        kv, dec = inp
        return dec[..., None] * carry + kv, carry

    init = jnp.zeros((b, H, DK, DV), f32)
    _, prev = lax.scan(step, init, (jnp.moveaxis(chunk_kv, 1, 0), jnp.moveaxis(chunk_decay, 1, 0)))
    prev = jnp.moveaxis(prev, 0, 1)
    o_inter = jnp.einsum("bclhd,bchdv->bclhv", q_in, prev)
    o = (o_intra + o_inter).reshape(b, s, H, DV)
    o = o * lax.rsqrt(jnp.mean(o * o, axis=-1, keepdims=True) + EPS) * norm_w.astype(f32)
    o = o * jax.nn.silu(g.astype(f32).reshape(b, s, H, DV))
    return o.reshape(b, s, GLA_VAL_DIM).astype(q.dtype) @ w_out


def setup_inputs(seed: int = 0) -> dict:
    key = jax.random.key(seed)
    ks = jax.random.split(key, 24)
    f32 = jnp.float32

    def nrm(k, shape, scale):
        return jax.random.normal(k, shape, f32) * scale

    dt0 = jnp.exp(jax.random.uniform(ks[7], (DEPTH, SSM_HEADS), f32, math.log(1e-3), math.log(1e-1)))
    return {
        "x": nrm(ks[0], (BATCH, SEQ, D_MODEL), 1.0),
        "norm_mix_w": 1.0 + nrm(ks[1], (DEPTH, D_MODEL), 0.02),
        "w_in": nrm(ks[2], (DEPTH, D_MODEL, IN_WIDTH), D_MODEL ** -0.5),
        "conv_a_w": nrm(ks[3], (DEPTH, SC_CONV_WIDTH, SC_WIDTH), SC_CONV_WIDTH ** -0.5),
        "w_out_a": nrm(ks[4], (DEPTH, SC_WIDTH, D_MODEL), SC_WIDTH ** -0.5),
        "ssm_conv_w": nrm(ks[5], (DEPTH, SSM_CONV_WIDTH, SSM_CONV_DIM), SSM_CONV_WIDTH ** -0.5),
        "ssm_conv_b": nrm(ks[6], (DEPTH, SSM_CONV_DIM), 0.02),
        "ssm_dt_bias": dt0 + jnp.log(-jnp.expm1(-dt0)),
        "ssm_a_log": jnp.log(jax.random.uniform(ks[8], (DEPTH, SSM_HEADS), f32, 1.0, 16.0)),
        "ssm_d": 1.0 + nrm(ks[9], (DEPTH, SSM_HEADS), 0.1),
        "ssm_norm_w": 1.0 + nrm(ks[10], (DEPTH, SSM_D_INNER), 0.02),
        "w_out_ssm": nrm(ks[11], (DEPTH, SSM_D_INNER, D_MODEL), SSM_D_INNER ** -0.5),
        "gla_w_gk2": nrm(ks[12], (DEPTH, GLA_GATE_RANK, GLA_KEY_DIM), GLA_GATE_RANK ** -0.5),
        "gla_b_gk": nrm(ks[13], (DEPTH, GLA_KEY_DIM), 0.1),
        "gla_norm_w": 1.0 + nrm(ks[14], (DEPTH, GLA_HEAD_V), 0.02),
        "w_out_gla": nrm(ks[15], (DEPTH, GLA_VAL_DIM, D_MODEL), GLA_VAL_DIM ** -0.5),
        "w_o": nrm(ks[16], (DEPTH, D_MODEL, D_MODEL), D_MODEL ** -0.5),
        "norm_mlp_w": 1.0 + nrm(ks[17], (DEPTH, D_MODEL), 0.02),
        "w_mlp_up": nrm(ks[18], (DEPTH, D_MODEL, D_FF), D_MODEL ** -0.5),
        "w_mlp_down": nrm(ks[19], (DEPTH, D_FF, D_MODEL), D_FF ** -0.5),
        "norm_f_w": 1.0 + nrm(ks[20], (D_MODEL,), 0.02),
    }


def reference(x, norm_mix_w, w_in, conv_a_w, w_out_a, ssm_conv_w, ssm_conv_b, ssm_dt_bias,
              ssm_a_log, ssm_d, ssm_norm_w, w_out_ssm, gla_w_gk2, gla_b_gk, gla_norm_w,
              w_out_gla, w_o, norm_mlp_w, w_mlp_up, w_mlp_down, norm_f_w):
    b, s, _ = x.shape
    for i in range(DEPTH):
        h = rms_norm(x, norm_mix_w[i])
        u = h @ w_in[i]
        (a_x, a_b, a_c, z, xbc, dt_raw, q, k, v, g, gk_low, gates) = split_cols(u, SPLIT_SIZES)
        y_a = short_conv_mixer(a_x, a_b, a_c, conv_a_w[i], w_out_a[i])
        y_b = mamba2_ssd(z, xbc, dt_raw, ssm_conv_w[i], ssm_conv_b[i], ssm_dt_bias[i],
                         ssm_a_log[i], ssm_d[i], ssm_norm_w[i], w_out_ssm[i])
        y_c = gla_mixer(q, k, v, g, gk_low, gla_w_gk2[i], gla_b_gk[i], gla_norm_w[i], w_out_gla[i])
        gt = jax.nn.sigmoid(gates.astype(jnp.float32)).reshape(b, s, N_BRANCHES, D_MODEL)
        merged = (gt[:, :, 0] * y_a.astype(jnp.float32) + gt[:, :, 1] * y_b.astype(jnp.float32)
                  + gt[:, :, 2] * y_c.astype(jnp.float32))
        x = x + merged.astype(x.dtype) @ w_o[i]
        h = rms_norm(x, norm_mlp_w[i])
        x = x + jnp.square(jax.nn.relu(h @ w_mlp_up[i])) @ w_mlp_down[i]
    return rms_norm(x, norm_f_w)
```

```python
import numpy as np
import concourse.bass as bass
import concourse.mybir as mybir
from concourse.bass_utils import run_bass_kernel_spmd

F32 = mybir.dt.float32
BF16 = mybir.dt.bfloat16
AF = mybir.ActivationFunctionType
ALU = mybir.AluOpType

D = 1024
INW = 12320
DFF = 4096
EPS = 1e-6
O_AX, O_AB, O_AC = 0, 1024, 2048
O_Z, O_XBC, O_DT = 3072, 4096, 6144
O_Q, O_K, O_V, O_G, O_GK = 6160, 6672, 7184, 8208, 9232
O_GATE = 9248

PP = {}
_o = 0
for _n, _w in [("nw_mix", 8), ("nw_mlp", 8), ("cwA", 24), ("cwB", 64), ("cbB", 16), ("dtb", 16),
               ("alog", 16), ("dsk", 16), ("snw", 1024), ("gnw", 256), ("wgk", 512)]:
    PP[_n] = (_o, _w)
    _o += _w
NPAR = _o
NCONST = 7 * 128 + 8


class Op:
    __slots__ = ("eng", "fn", "deps", "dma_sem", "token", "signal", "is_dma")

    def __init__(self, eng, fn, deps, is_dma=False):
        self.eng = eng
        self.fn = fn
        self.deps = deps
        self.is_dma = is_dma
        self.dma_sem = None
        self.token = None
        self.signal = False


class Buf:
    __slots__ = ("w", "r", "name", "excl")

    def __init__(self, name=""):
        self.w = None
        self.r = {}
        self.name = name
        self.excl = False


class Tile(Buf):
    __slots__ = ("ap",)

    def __init__(self, ap, name=""):
        Buf.__init__(self, name)
        self.ap = ap


class Prog:
    ENGS = ("pe", "act", "dve", "pool", "sp")

    def __init__(self, nc, same_engine_sync=True):
        self.nc = nc
        self.ops = {e: [] for e in self.ENGS}
        self.last = {e: None for e in self.ENGS}
        self.barrier_deps = []
        self.dmas_since_barrier = []
        self.same_engine_sync = same_engine_sync
        self.n_dma_sems = 0

    def op(self, eng, fn, r=(), w=(), is_dma=False):
        xr = [b for b in r if b.excl]
        if xr:
            r = [b for b in r if not b.excl]
            w = list(w) + [b for b in xr if b not in w]
        deps = {}
        for d in self.barrier_deps:
            deps[id(d)] = d
        for b in r:
            if b.w is not None:
                deps[id(b.w)] = b.w
        for b in w:
            if b.w is not None:
                deps[id(b.w)] = b.w
            for o in b.r.values():
                deps[id(o)] = o
        o = Op(eng, fn, list(deps.values()), is_dma)
        for b in r:
            b.r[eng if not is_dma else ("dma", id(o))] = o
        for b in w:
            b.w = o
            b.r = {}
        self.ops[eng].append(o)
        self.last[eng] = o
        if is_dma:
            self.dmas_since_barrier.append(o)
        return o

    def pe(self, fn, r=(), w=()):
        return self.op("pe", fn, r, w)

    def act(self, fn, r=(), w=()):
        return self.op("act", fn, r, w)

    def dve(self, fn, r=(), w=()):
        return self.op("dve", fn, r, w)

    def dma(self, eng, fn, r=(), w=()):
        return self.op(eng, fn, r, w, is_dma=True)

    def barrier(self):
        deps = [o for o in self.last.values() if o is not None and not o.is_dma]
        deps += self.dmas_since_barrier
        self.barrier_deps = deps
        self.dmas_since_barrier = []

    def emit_block(self, final_wait_ops):
        nc = self.nc
        for e in self.ENGS:
            for o in self.ops[e]:
                for d in o.deps:
                    if d.is_dma or d.eng != o.eng or (d.eng != "pe" and self.same_engine_sync):
                        d.signal = True
        for o in final_wait_ops:
            o.signal = True
        from contextlib import ExitStack
        with ExitStack() as st:
            esem = {e: st.enter_context(nc.semaphore("se_" + e)) for e in self.ENGS}
            NDS = 48
            dsem = [st.enter_context(nc.semaphore("sd%d" % i)) for i in range(NDS)]
            for e in self.ENGS:
                cnt = 0
                for o in self.ops[e]:
                    if o.is_dma:
                        continue
                    if o.signal:
                        cnt += 1
                        o.token = (esem[e], cnt)
            dcount = [0] * NDS
            dlast = [None] * NDS
            di = 0
            order = []
            for e in self.ENGS:
                for o in self.ops[e]:
                    if o.is_dma:
                        order.append(o)
            per_eng = {"sp": list(range(0, 16)), "pool": list(range(16, 48)), "act": [], "pe": [], "dve": []}
            rr = {e: 0 for e in self.ENGS}
            for e in self.ENGS:
                for o in self.ops[e]:
                    if not o.is_dma:
                        continue
                    lst = per_eng[e]
                    s = lst[rr[e] % len(lst)]
                    rr[e] += 1
                    dcount[s] += 1
                    o.token = (dsem[s], 16 * dcount[s])
                    if dlast[s] is not None:
                        o.deps.append(dlast[s])
                    dlast[s] = o
            blk = st.enter_context(nc.Block())

            def run(eng_name, eng):
                waited = {}
                for o in self.ops[eng_name]:
                    for d in o.deps:
                        if not d.is_dma and d.eng == eng_name and (eng_name == "pe" or not self.same_engine_sync):
                            continue
                        sem, val = d.token
                        k = id(sem)
                        if waited.get(k, 0) >= val:
                            continue
                        waited[k] = val
                        eng.wait_ge(sem, val)
                    ins = o.fn(eng)
                    if o.is_dma or o.signal:
                        sem, val = o.token
                        ins.then_inc(sem, 16 if o.is_dma else 1)
                if eng_name == "sp":
                    for o in final_wait_ops:
                        sem, val = o.token
                        eng.wait_ge(sem, val)

            @blk.tensor
            def _(e):
                run("pe", e)

            @blk.scalar
            def _(e):
                run("act", e)

            @blk.vector
            def _(e):
                run("dve", e)

            @blk.gpsimd
            def _(e):
                run("pool", e)

            @blk.sync
            def _(e):
                run("sp", e)


class Arena:
    def __init__(self, nc, name, nfloats):
        self.t = nc.alloc_sbuf_tensor(name, [128, nfloats], F32)
        self.n = nfloats
        self.off = 0

    def reset(self):
        self.off = 0

    def alloc(self, free_shape, dtype=F32, name=""):
        n = int(np.prod(free_shape))
        nf = n if dtype == F32 else (n + 1) // 2
        assert self.off + nf <= self.n, ("arena overflow", name, self.off, nf, self.n)
        a = self.t[:, self.off:self.off + nf]
        self.off += nf
        if dtype != F32:
            a = a.bitcast(dtype)
            if n % 2:
                a = a[:, 0:n]
        if len(free_shape) == 2:
            a = a.rearrange("p (a b) -> p a b", a=free_shape[0])
        elif len(free_shape) == 3:
            a = a.rearrange("p (a b c) -> p a b c", a=free_shape[0], b=free_shape[1])
        return Tile(a, name)


def build(NTILE, T, L=2):
    assert T % 128 == 0 and T <= 512
    NS = T // 128
    S = NTILE * T
    nc = bass.Bass("TRN2", target_bir_lowering=False)
    x_d = nc.dram_tensor("x", [S, D], F32, kind="ExternalInput").ap()
    out_d = nc.dram_tensor("out", [S, D], F32, kind="ExternalOutput").ap()
    w_in = nc.dram_tensor("w_in", [L, D, INW], F32, kind="ExternalInput").ap()
    w_oa = nc.dram_tensor("w_out_a", [L, D, D], F32, kind="ExternalInput").ap()
    w_os = nc.dram_tensor("w_out_ssm", [L, D, D], F32, kind="ExternalInput").ap()
    w_og = nc.dram_tensor("w_out_gla", [L, D, D], F32, kind="ExternalInput").ap()
    w_o = nc.dram_tensor("w_o", [L, D, D], F32, kind="ExternalInput").ap()
    w_up = nc.dram_tensor("w_mlp_up", [L, D, DFF], F32, kind="ExternalInput").ap()
    w_dn = nc.dram_tensor("w_mlp_down", [L, DFF, D], F32, kind="ExternalInput").ap()
    pp_d = nc.dram_tensor("ppack", [L, 128, NPAR], F32, kind="ExternalInput").ap()
    cc_d = nc.dram_tensor("cpack", [128, NCONST], F32, kind="ExternalInput").ap()

    P = Prog(nc)
    final_stores = []

    def sb(name, shape, dt=F32):
        return Tile(nc.alloc_sbuf_tensor(name, shape, dt)[:], name)

    xT = [sb("xT%d" % k, [128, T]) for k in range(8)]
    hT = [sb("hT%d" % k, [128, T], BF16) for k in range(8)]
    mg = [sb("mg%d" % k, [128, T]) for k in range(8)]
    NWS = 3
    wslot = [sb("wsl%d" % i, [128, 4096], BF16) for i in range(NWS)]
    wsmall = [sb("wsm%d" % i, [128, 8, 16], BF16) for i in range(2)]
    pp = [sb("pp%d" % l, [128, NPAR]) for l in range(L)]
    cc = sb("cc", [128, NCONST])
    identb = sb("identb", [128, 128], BF16)
    negA = [sb("negA%d" % l, [128, 16]) for l in range(L)]
    Sst = [sb("Sst%d" % l, [128, 1024]) for l in range(L)]
    Sstb = [sb("Sstb%d" % l, [128, 1024], BF16) for l in range(L)]
    Gst = [sb("Gst%d" % l, [128, 1024]) for l in range(L)]
    Gstb = [sb("Gstb%d" % l, [128, 1024], BF16) for l in range(L)]
    haloA = [sb("haloA%d" % l, [128, 8, 2]) for l in range(L)]
    haloB = [sb("haloB%d" % l, [128, 16, 3]) for l in range(L)]
    AR = Arena(nc, "arena", 20000)
    psb = [Tile(nc.alloc_psum_tensor("ps%d" % i, [128, 512], F32)[:], "ps%d" % i) for i in range(8)]
    pcnt = [0]

    def bank():
        b = psb[pcnt[0] % 8]
        pcnt[0] += 1
        return b

    def C(name):
        i = ["ident", "ones", "tri", "gt", "blktri", "blkgt", "maskg"].index(name)
        return cc.ap[:, i * 128:(i + 1) * 128]

    nwf = cc.ap[:, 7 * 128:7 * 128 + 8]

    def PPs(l, name):
        o, w = PP[name]
        return pp[l].ap[:, o:o + w]

    def mm(out, lhsT, rhs, start, stop, r, w):
        return P.pe(lambda e: e.matmul(out, lhsT, rhs, start=start, stop=stop), r, w)

    def act(out, in_, func, r, w, bias=None, scale=None, accum=None):
        kw = {}
        if bias is not None:
            kw["bias"] = bias
        if scale is not None:
            kw["scale"] = scale
        if accum is not None:
            kw["accum_out"] = accum
        return P.act(lambda e: e.activation(out=out, in_=in_, func=func, **kw), r, w)

    def tt(out, a, b, op, r, w):
        return P.dve(lambda e: e.tensor_tensor(out=out, in0=a, in1=b, op=op), r, w)

    def stt(out, in0, scalar, in1, op0, op1, r, w):
        return P.dve(lambda e: e.scalar_tensor_tensor(out=out, in0=in0, scalar=scalar, in1=in1, op0=op0, op1=op1), r, w)

    def ts(out, in0, s1, s2, op0, op1, r, w):
        if s2 is None:
            return P.dve(lambda e: e.tensor_scalar(out=out, in0=in0, scalar1=s1, scalar2=None, op0=op0), r, w)
        return P.dve(lambda e: e.tensor_scalar(out=out, in0=in0, scalar1=s1, scalar2=s2, op0=op0, op1=op1), r, w)

    def vcopy(out, in_, r, w):
        return P.dve(lambda e: e.tensor_copy(out=out, in_=in_), r, w)

    def acopy(out, in_, r, w):
        return P.act(lambda e: e.activation(out=out, in_=in_, func=AF.Copy), r, w)

    def memset(ap, val, w):
        return P.dve(lambda e: e.memset(ap, val), (), w)

    wctr = [0]

    def wload(src_aps, view):
        sl = wslot[wctr[0] % NWS]
        wctr[0] += 1
        v = view(sl.ap)
        for (src, dst) in src_aps:
            d = dst(v)
            P.dma("pool", lambda e, d=d, src=src: e.dma_start(out=d, in_=src), (), [sl])
        return sl, v

    def win_cols(l, c0, n):
        return w_in[l].rearrange("(k p) c -> p k c", p=128)[:, :, c0:c0 + n]

    def wsq_cols(wd, l, c0, n):
        return wd[l].rearrange("(k p) c -> p k c", p=128)[:, :, c0:c0 + n]

    smctr = [0]

    def wload_small(l, c0):
        sl = wsmall[smctr[0] % 2]
        smctr[0] += 1
        src = win_cols(l, c0, 16)
        P.dma("pool", lambda e: e.dma_start(out=sl.ap, in_=src), (), [sl])
        return sl

    P.dma("sp", lambda e: e.dma_start(out=cc.ap, in_=cc_d), (), [cc])
    for l in range(L):
        P.dma("sp", lambda e, l=l: e.dma_start(out=pp[l].ap, in_=pp_d[l]), (), [pp[l]])
    vcopy(identb.ap, C("ident"), [cc], [identb])
    for l in range(L):
        memset(Sst[l].ap, 0.0, [Sst[l]])
        memset(Sstb[l].ap, 0.0, [Sstb[l]])
        memset(Gst[l].ap, 0.0, [Gst[l]])
        memset(Gstb[l].ap, 0.0, [Gstb[l]])
        memset(haloA[l].ap, 0.0, [haloA[l]])
        memset(haloB[l].ap, 0.0, [haloB[l]])
        act(negA[l].ap, PPs(l, "alog"), AF.Exp, [pp[l]], [negA[l]])
        ts(negA[l].ap, negA[l].ap, -1.0, None, ALU.mult, None, [negA[l]], [negA[l]])

    def rmsnorm_to_hT(nw_ap, nw_buf):
        AR.reset()
        sq = [AR.alloc([T], F32, "sq%d" % i) for i in range(2)]
        rs = AR.alloc([T], F32, "rstd")
        b = bank()
        for k in range(8):
            s = sq[k % 2]
            act(s.ap, xT[k].ap, AF.Square, [xT[k]], [s])
            mm(b.ap[:, 0:T], C("ones"), s.ap, k == 0, k == 7, [cc, s], [b])
        ts(rs.ap, b.ap[:, 0:T], 1.0 / D, EPS, ALU.mult, ALU.add, [b], [rs])
        act(rs.ap, rs.ap, AF.Ln, [rs], [rs])
        act(rs.ap, rs.ap, AF.Exp, [rs], [rs], scale=-0.5)
        return rs

    def apply_norm(rs, nw_ap, nw_buf, outs):
        for k in range(8):
            stt(outs[k].ap, xT[k].ap, nw_ap[:, k:k + 1], rs.ap, ALU.mult, ALU.mult, [xT[k], nw_buf, rs], [outs[k]])

    for ti in range(NTILE):
        tok0 = ti * T
        P.barrier()
        AR.reset()
        xin = AR.alloc([NS, D], F32, "xin")
        P.dma("sp", lambda e, xin=xin, tok0=tok0: e.dma_start(
            out=xin.ap, in_=x_d[tok0:tok0 + T, :].rearrange("(s p) d -> p s d", p=128)), (), [xin])
        for k in range(8):
            b = bank()
            for s in range(NS):
                mm(b.ap[:, s * 128:(s + 1) * 128], xin.ap[:, s, k * 128:(k + 1) * 128], C("ident"), True, True, [xin, cc], [b])
            vcopy(xT[k].ap, b.ap[:, 0:T], [b], [xT[k]])

        for l in range(L):
            P.barrier()
            rs = rmsnorm_to_hT(None, None)
            apply_norm(rs, PPs(l, "nw_mix"), pp[l], hT)

            P.barrier()
            AR.reset()
            ub = [AR.alloc([T], BF16, "ub%d" % c) for c in range(8)]
            pbuf = [AR.alloc([T + 2], F32, "p%d" % i) for i in range(2)]
            axs = [AR.alloc([T], F32, "axs%d" % i) for i in range(2)]
            u0 = [AR.alloc([T], F32, "u0%d" % i) for i in range(2)]
            u1 = [AR.alloc([T], F32, "u1%d" % i) for i in range(2)]
            cw = PPs(l, "cwA")
            for c in range(8):
                sl, v = wload(
                    [(win_cols(l, O_AX + c * 128, 128), lambda v: v[:, :, 0:128]),
                     (win_cols(l, O_AB + c * 128, 128), lambda v: v[:, :, 128:256]),
                     (win_cols(l, O_AC + c * 128, 128), lambda v: v[:, :, 256:384])],
                    lambda a: a[:, 0:8 * 384].rearrange("p (k c) -> p k c", k=8))
                bx, bb, bc = bank(), bank(), bank()
                for (bk, co) in ((bx, 0), (bb, 128), (bc, 256)):
                    for k in range(8):
                        mm(bk.ap[:, 0:T], v[:, k, co:co + 128], hT[k].ap, k == 0, k == 7, [sl, hT[k]], [bk])
                pb = pbuf[c % 2]
                ax = axs[c % 2]
                a0 = u0[c % 2]
                a1 = u1[c % 2]
                acopy(ax.ap, bx.ap[:, 0:T], [bx], [ax])
                vcopy(pb.ap[:, 0:2], haloA[l].ap[:, c, :], [haloA[l]], [pb])
                tt(pb.ap[:, 2:2 + T], bc.ap[:, 0:T], ax.ap, ALU.mult, [bc, ax], [pb])
                vcopy(haloA[l].ap[:, c, :], pb.ap[:, T:T + 2], [pb], [haloA[l]])
                act(a0.ap, pb.ap[:, 0:T], AF.Identity, [pb, pp[l]], [a0], scale=cw[:, c * 3:c * 3 + 1])
                stt(a1.ap, pb.ap[:, 1:1 + T], cw[:, c * 3 + 1:c * 3 + 2], a0.ap, ALU.mult, ALU.add, [pb, a0, pp[l]], [a1])
                stt(a0.ap, pb.ap[:, 2:2 + T], cw[:, c * 3 + 2:c * 3 + 3], a1.ap, ALU.mult, ALU.add, [pb, a1, pp[l]], [a0])
                tt(ub[c].ap, a0.ap, bb.ap[:, 0:T], ALU.mult, [a0, bb], [ub[c]])

            def outproj_gate(wd, act_in, br, first):
                sg = [AR.alloc([T], F32, "sg%d" % i) for i in range(2)]
                tm = [AR.alloc([T], F32, "tm%d" % i) for i in range(2)]
                for j in range(8):
                    sl, v = wload(
                        [(wsq_cols(wd, l, j * 128, 128), lambda v: v[:, :, 0:128]),
                         (win_cols(l, O_GATE + br * 1024 + j * 128, 128), lambda v: v[:, :, 128:256])],
                        lambda a: a[:, 0:8 * 256].rearrange("p (k c) -> p k c", k=8))
                    by, bg = bank(), bank()
                    for k in range(8):
                        mm(by.ap[:, 0:T], v[:, k, 0:128], act_in[k].ap, k == 0, k == 7, [sl, act_in[k]], [by])
                    for k in range(8):
                        mm(bg.ap[:, 0:T], v[:, k, 128:256], hT[k].ap, k == 0, k == 7, [sl, hT[k]], [bg])
                    s = sg[j % 2]
                    act(s.ap, bg.ap[:, 0:T], AF.Sigmoid, [bg], [s])
                    if first:
                        tt(mg[j].ap, s.ap, by.ap[:, 0:T], ALU.mult, [s, by], [mg[j]])
                    else:
                        t_ = tm[j % 2]
                        tt(t_.ap, s.ap, by.ap[:, 0:T], ALU.mult, [s, by], [t_])
                        tt(mg[j].ap, mg[j].ap, t_.ap, ALU.add, [mg[j], t_], [mg[j]])

            outproj_gate(w_oa, ub, 0, True)

            P.barrier()
            AR.reset()
            xs_fm = [AR.alloc([T], BF16, "xsfm%d" % c) for c in range(8)]
            BmT = [AR.alloc([T], BF16, "BmT%d" % g) for g in range(4)]
            CmT = [AR.alloc([T], BF16, "CmT%d" % g) for g in range(4)]
            sz = AR.alloc([NS, 1024], F32, "sz")
            yT = [AR.alloc([T], BF16, "yT%d" % c) for c in range(8)]
            dtr = AR.alloc([NS, 16], F32, "dtraw")
            mark = AR.off
            xraw = [AR.alloc([T + 3], F32, "xraw%d" % i) for i in range(2)]
            c0 = [AR.alloc([T], F32, "c0%d" % i) for i in range(2)]
            c1 = [AR.alloc([T], F32, "c1%d" % i) for i in range(2)]
            cwB = PPs(l, "cwB")
            cbB = PPs(l, "cbB")
            for pc in range(4):
                sl, v = wload([(win_cols(l, O_XBC + pc * 512, 512), lambda v: v)],
                              lambda a: a.rearrange("p (k c) -> p k c", k=8))
                for cc_ in range(4):
                    c = pc * 4 + cc_
                    b = bank()
                    for k in range(8):
                        mm(b.ap[:, 0:T], v[:, k, cc_ * 128:(cc_ + 1) * 128], hT[k].ap, k == 0, k == 7, [sl, hT[k]], [b])
                    xr = xraw[c % 2]
                    a0 = c0[c % 2]
                    a1 = c1[c % 2]
                    vcopy(xr.ap[:, 0:3], haloB[l].ap[:, c, :], [haloB[l]], [xr])
                    acopy(xr.ap[:, 3:3 + T], b.ap[:, 0:T], [b], [xr])
                    vcopy(haloB[l].ap[:, c, :], xr.ap[:, T:T + 3], [xr], [haloB[l]])
                    act(a0.ap, xr.ap[:, 0:T], AF.Identity, [xr, pp[l]], [a0], scale=cwB[:, c * 4:c * 4 + 1], bias=cbB[:, c:c + 1])
                    stt(a1.ap, xr.ap[:, 1:1 + T], cwB[:, c * 4 + 1:c * 4 + 2], a0.ap, ALU.mult, ALU.add, [xr, a0, pp[l]], [a1])
                    stt(a0.ap, xr.ap[:, 2:2 + T], cwB[:, c * 4 + 2:c * 4 + 3], a1.ap, ALU.mult, ALU.add, [xr, a1, pp[l]], [a0])
                    stt(a1.ap, xr.ap[:, 3:3 + T], cwB[:, c * 4 + 3:c * 4 + 4], a0.ap, ALU.mult, ALU.add, [xr, a0, pp[l]], [a1])
                    dst = xs_fm[c] if c < 8 else (BmT[c - 8] if c < 12 else CmT[c - 12])
                    act(dst.ap, a1.ap, AF.Silu, [a1], [dst])
            for pc in range(2):
                sl, v = wload([(win_cols(l, O_Z + pc * 512, 512), lambda v: v)],
                              lambda a: a.rearrange("p (k c) -> p k c", k=8))
                for s in range(NS):
                    b = bank()
                    for k in range(8):
                        mm(b.ap, hT[k].ap[:, s * 128:(s + 1) * 128], v[:, k, :], k == 0, k == 7, [sl, hT[k]], [b])
                    act(sz.ap[:, s, pc * 512:(pc + 1) * 512], b.ap, AF.Silu, [b], [sz])
            sm = wload_small(l, O_DT)
            for s in range(NS):
                b = bank()
                for k in range(8):
                    mm(b.ap[:, 0:16], hT[k].ap[:, s * 128:(s + 1) * 128], sm.ap[:, k, :], k == 0, k == 7, [sm, hT[k]], [b])
                tt(dtr.ap[:, s, :], b.ap[:, 0:16], PPs(l, "dtb"), ALU.add, [b, pp[l]], [dtr])
            act(dtr.ap, dtr.ap, AF.Exp, [dtr], [dtr])
            act(dtr.ap, dtr.ap, AF.Ln, [dtr], [dtr], bias=1.0)

            P.barrier()
            AR.off = mark
            dtA = AR.alloc([16], F32, "dtA")
            acs = AR.alloc([32], F32, "acs")
            eac = AR.alloc([16], F32, "eac")
            cd = AR.alloc([16], F32, "cd")
            dte = AR.alloc([16], F32, "dte")
            xdt = AR.alloc([1024], BF16, "xdt")
            xstm = AR.alloc([1024], F32, "xstm")
            xdte = AR.alloc([1024], BF16, "xdte")
            Bmtm = AR.alloc([512], BF16, "Bmtm")
            CBTm = AR.alloc([512], F32, "CBTm")
            Lh = [AR.alloc([128], F32, "Lh%d" % i) for i in range(4)]
            Ee = [AR.alloc([512], F32, "Ee%d" % i) for i in range(2)]
            MT = [AR.alloc([512], BF16, "MT%d" % g) for g in range(4)]
            yv = AR.alloc([1024], F32, "yv")
            yv2 = AR.alloc([1024], F32, "yv2")
            ynb = AR.alloc([1024], BF16, "ynb")
            ss4 = AR.alloc([4], F32, "ss4")
            junk = AR.alloc([256], F32, "junk")
            stmp = AR.alloc([1024], F32, "stmp")
            dsk = PPs(l, "dsk")
            snw = PPs(l, "snw")
            for s in range(NS):
                tsl = slice(s * 128, (s + 1) * 128)
                tt(dtA.ap, dtr.ap[:, s, :], negA[l].ap, ALU.mult, [dtr, negA[l]], [dtA])
                bA = bank()
                mm(bA.ap[:, 0:16], C("tri"), dtA.ap, True, True, [cc, dtA], [bA])
                mm(bA.ap[:, 16:32], C("ones"), dtA.ap, True, True, [cc, dtA], [bA])
                vcopy(acs.ap, bA.ap[:, 0:32], [bA], [acs])
                act(eac.ap, acs.ap[:, 0:16], AF.Exp, [acs], [eac])
                act(cd.ap, acs.ap[:, 16:32], AF.Exp, [acs], [cd])
                tt(dte.ap, acs.ap[:, 16:32], acs.ap[:, 0:16], ALU.subtract, [acs], [dte])
                act(dte.ap, dte.ap, AF.Exp, [dte], [dte])
                for hb in range(2):
                    b = bank()
                    for c4 in range(4):
                        c = hb * 4 + c4
                        mm(b.ap[:, c4 * 128:(c4 + 1) * 128], xs_fm[c].ap[:, tsl], identb.ap, True, True, [xs_fm[c], identb], [b])
                    hs = slice(hb * 512, (hb + 1) * 512)
                    vcopy(xstm.ap[:, hs], b.ap, [b], [xstm])
                    tt(xdt.ap[:, hs].rearrange("p (h q) -> p h q", q=64), b.ap.rearrange("p (h q) -> p h q", q=64),
                       dtr.ap[:, s, hb * 8:(hb + 1) * 8].unsqueeze(2).to_broadcast([128, 8, 64]), ALU.mult, [b, dtr], [xdt])
                tt(xdte.ap.rearrange("p (h q) -> p h q", q=64), xdt.ap.rearrange("p (h q) -> p h q", q=64),
                   dte.ap.unsqueeze(2).to_broadcast([128, 16, 64]), ALU.mult, [xdt, dte], [xdte])
                b = bank()
                for g in range(4):
                    mm(b.ap[:, g * 128:(g + 1) * 128], BmT[g].ap[:, tsl], identb.ap, True, True, [BmT[g], identb], [b])
                acopy(Bmtm.ap, b.ap, [b], [Bmtm])
                b = bank()
                for g in range(4):
                    mm(b.ap[:, g * 128:(g + 1) * 128], BmT[g].ap[:, tsl], CmT[g].ap[:, tsl], True, True, [BmT[g], CmT[g]], [b])
                tt(CBTm.ap.rearrange("p (g l) -> p g l", g=4), b.ap.rearrange("p (g l) -> p g l", g=4),
                   C("tri").unsqueeze(1).to_broadcast([128, 4, 128]), ALU.mult, [b, cc], [CBTm])
                for g in range(4):
                    b = bank()
                    for r_ in range(4):
                        h = g * 4 + r_
                        lh = Lh[r_]
                        ts(lh.ap, C("gt"), dtA.ap[:, h:h + 1], None, ALU.mult, None, [cc, dtA], [lh])
                        mm(b.ap[:, r_ * 128:(r_ + 1) * 128], lh.ap, C("tri"), True, True, [lh, cc], [b])
                    ee = Ee[g % 2]
                    act(ee.ap, b.ap, AF.Exp, [b], [ee])
                    tt(MT[g].ap.rearrange("p (r l) -> p r l", r=4), ee.ap.rearrange("p (r l) -> p r l", r=4),
                       CBTm.ap[:, g * 128:(g + 1) * 128].unsqueeze(1).to_broadcast([128, 4, 128]), ALU.mult, [ee, CBTm], [MT[g]])
                bd = [bank(), bank()]
                bo = [bank(), bank()]
                for h in range(16):
                    g, r_ = h // 4, h % 4
                    bb_ = bd[h // 8]
                    mm(bb_.ap[:, (h % 8) * 64:(h % 8 + 1) * 64], MT[g].ap[:, r_ * 128:(r_ + 1) * 128], xdt.ap[:, h * 64:(h + 1) * 64],
                       True, True, [MT[g], xdt], [bb_])
                for g in range(4):
                    bb_ = bo[g // 2]
                    mm(bb_.ap[:, (g % 2) * 256:(g % 2 + 1) * 256], CmT[g].ap[:, tsl], Sstb[l].ap[:, g * 256:(g + 1) * 256],
                       True, True, [CmT[g], Sstb[l]], [bb_])
                for hb in range(2):
                    hs = slice(hb * 512, (hb + 1) * 512)
                    v3 = lambda a: a.rearrange("p (h q) -> p h q", q=64)
                    tt(v3(yv.ap[:, hs]), v3(bo[hb].ap), eac.ap[:, hb * 8:(hb + 1) * 8].unsqueeze(2).to_broadcast([128, 8, 64]),
                       ALU.mult, [bo[hb], eac], [yv])
                    tt(yv.ap[:, hs], yv.ap[:, hs], bd[hb].ap, ALU.add, [yv, bd[hb]], [yv])
                    tt(v3(yv2.ap[:, hs]), v3(xstm.ap[:, hs]), dsk[:, hb * 8:(hb + 1) * 8].unsqueeze(2).to_broadcast([128, 8, 64]),
                       ALU.mult, [xstm, pp[l]], [yv2])
                    tt(yv.ap[:, hs], yv.ap[:, hs], yv2.ap[:, hs], ALU.add, [yv, yv2], [yv])
                    tt(yv.ap[:, hs], yv.ap[:, hs], sz.ap[:, s, hs], ALU.mult, [yv, sz], [yv])
                bs = [bank(), bank()]
                for g in range(4):
                    bb_ = bs[g // 2]
                    mm(bb_.ap[:, (g % 2) * 256:(g % 2 + 1) * 256], Bmtm.ap[:, g * 128:(g + 1) * 128], xdte.ap[:, g * 256:(g + 1) * 256],
                       True, True, [Bmtm, xdte], [bb_])
                for hb in range(2):
                    hs = slice(hb * 512, (hb + 1) * 512)
                    v3 = lambda a: a.rearrange("p (h q) -> p h q", q=64)
                    tt(v3(stmp.ap[:, hs]), v3(Sst[l].ap[:, hs]), cd.ap[:, hb * 8:(hb + 1) * 8].unsqueeze(2).to_broadcast([128, 8, 64]),
                       ALU.mult, [Sst[l], cd], [stmp])
                    tt(Sst[l].ap[:, hs], stmp.ap[:, hs], bs[hb].ap, ALU.add, [stmp, bs[hb]], [Sst[l]])
                acopy(Sstb[l].ap, Sst[l].ap, [Sst[l]], [Sstb[l]])
                memset(ss4.ap, 0.0, [ss4])
                for g in range(4):
                    act(junk.ap, yv.ap[:, g * 256:(g + 1) * 256], AF.Square, [yv], [junk, ss4], accum=ss4.ap[:, g:g + 1])
                ts(ss4.ap, ss4.ap, 1.0 / 256, EPS, ALU.mult, ALU.add, [ss4], [ss4])
                act(ss4.ap, ss4.ap, AF.Ln, [ss4], [ss4])
                act(ss4.ap, ss4.ap, AF.Exp, [ss4], [ss4], scale=-0.5)
                tt(yv.ap.rearrange("p (g q) -> p g q", g=4), yv.ap.rearrange("p (g q) -> p g q", g=4),
                   ss4.ap.unsqueeze(2).to_broadcast([128, 4, 256]), ALU.mult, [yv, ss4], [yv])
                tt(ynb.ap, yv.ap, snw, ALU.mult, [yv, pp[l]], [ynb])
                for hb in range(2):
                    b = bank()
                    for c4 in range(4):
                        c = hb * 4 + c4
                        mm(b.ap[:, c4 * 128:(c4 + 1) * 128], ynb.ap[:, c * 128:(c + 1) * 128], identb.ap, True, True, [ynb, identb], [b])
                    for c4 in range(4):
                        c = hb * 4 + c4
                        acopy(yT[c].ap[:, tsl], b.ap[:, c4 * 128:(c4 + 1) * 128], [b], [yT[c]])
            P.barrier()
            AR.off = mark
            outproj_gate(w_os, yT, 1, False)

            P.barrier()
            AR.reset()
            qin = [AR.alloc([T], BF16, "qin%d" % h) for h in range(4)]
            kin = [AR.alloc([T], BF16, "kin%d" % h) for h in range(4)]
            egl = AR.alloc([4, T // 64], F32, "egl")
            gkl = AR.alloc([T], F32, "gkl")
            gkpos = AR.alloc([NS, 512], F32, "gkpos")
            ktm = AR.alloc([NS, 512], F32, "ktm")
            vbf = AR.alloc([NS, 1024], BF16, "vbf")
            sgt = AR.alloc([NS, 1024], F32, "sgt")
            oT = [AR.alloc([T], BF16, "oT%d" % c) for c in range(8)]
            mark = AR.off
            sm = wload_small(l, O_GK)
            memset(gkl.ap[0:32, :], 1.0, [gkl])
            b = bank()
            for k in range(8):
                mm(b.ap[0:16, 0:T], sm.ap[:, k, :], hT[k].ap, k == 0, k == 7, [sm, hT[k]], [b])
            vcopy(gkl.ap[0:16, :], b.ap[0:16, 0:T], [b], [gkl])
            wgk = PPs(l, "wgk")
            for s in range(NS):
                b = bank()
                mm(b.ap, gkl.ap[0:17, s * 128:(s + 1) * 128], wgk[0:17, :], True, True, [gkl, pp[l]], [b])
                act(gkpos.ap[:, s, :], b.ap, AF.Exp, [b], [gkpos], scale=-1.0)
            act(gkpos.ap, gkpos.ap, AF.Ln, [gkpos], [gkpos], bias=1.0)
            eq = [AR.alloc([T], F32, "eq%d" % i) for i in range(2)]
            ek = [AR.alloc([T], F32, "ek%d" % i) for i in range(2)]
            for (pc, off, dst) in ((0, O_Q, qin), (1, O_K, kin)):
                sl, v = wload([(win_cols(l, off, 512), lambda v: v)], lambda a: a.rearrange("p (k c) -> p k c", k=8))
                for h in range(4):
                    b = bank()
                    for k in range(8):
                        mm(b.ap[:, 0:T], v[:, k, h * 128:(h + 1) * 128], hT[k].ap, k == 0, k == 7, [sl, hT[k]], [b])
                    bg = bank()
                    for s in range(NS):
                        mm(bg.ap[:, s * 128:(s + 1) * 128], gkpos.ap[:, s, h * 128:(h + 1) * 128], C("blktri"), True, True, [gkpos, cc], [bg])
                    if pc == 0:
                        e_ = eq[h % 2]
                        act(e_.ap, bg.ap[:, 0:T], AF.Exp, [bg], [e_])
                        vcopy(egl.ap[:, h, :], e_.ap[:, 63:T:64], [e_], [egl])
                        stt(qin[h].ap, b.ap[:, 0:T], float(128 ** -0.5), e_.ap, ALU.mult, ALU.mult, [b, e_], [qin[h]])
                    else:
                        e_ = ek[h % 2]
                        act(e_.ap, bg.ap[:, 0:T], AF.Exp, [bg], [e_], scale=-1.0)
                        tt(kin[h].ap, b.ap[:, 0:T], e_.ap, ALU.mult, [b, e_], [kin[h]])
            for (off, npc, kind) in ((O_K, 1, "k"), (O_V, 2, "v"), (O_G, 2, "g")):
                for pc in range(npc):
                    sl, v = wload([(win_cols(l, off + pc * 512, 512), lambda v: v)], lambda a: a.rearrange("p (k c) -> p k c", k=8))
                    for s in range(NS):
                        b = bank()
                        for k in range(8):
                            mm(b.ap, hT[k].ap[:, s * 128:(s + 1) * 128], v[:, k, :], k == 0, k == 7, [sl, hT[k]], [b])
                        if kind == "k":
                            vcopy(ktm.ap[:, s, :], b.ap, [b], [ktm])
                        elif kind == "v":
                            acopy(vbf.ap[:, s, pc * 512:(pc + 1) * 512], b.ap, [b], [vbf])
                        else:
                            act(sgt.ap[:, s, pc * 512:(pc + 1) * 512], b.ap, AF.Silu, [b], [sgt])
            P.barrier()
            AR.off = mark
            kend = AR.alloc([512], BF16, "kend")
            ekd = AR.alloc([512], F32, "ekd")
            scm = AR.alloc([512], BF16, "scm")
            ov = AR.alloc([1024], F32, "ov")
            onb = AR.alloc([1024], BF16, "onb")
            ss4 = AR.alloc([4], F32, "ss4g")
            junk = AR.alloc([256], F32, "junkg")
            gnw = PPs(l, "gnw")
            for s in range(NS):
                tsl = slice(s * 128, (s + 1) * 128)
                b = bank()
                mm(b.ap, C("blkgt"), gkpos.ap[:, s, :], True, True, [cc, gkpos], [b])
                act(ekd.ap, b.ap, AF.Exp, [b], [ekd])
                tt(kend.ap, ktm.ap[:, s, :], ekd.ap, ALU.mult, [ktm, ekd], [kend])
                b = bank()
                for h in range(4):
                    mm(b.ap[:, h * 128:(h + 1) * 128], kin[h].ap[:, tsl], qin[h].ap[:, tsl], True, True, [kin[h], qin[h]], [b])
                tt(scm.ap.rearrange("p (h l) -> p h l", h=4), b.ap.rearrange("p (h l) -> p h l", h=4),
                   C("maskg").unsqueeze(1).to_broadcast([128, 4, 128]), ALU.mult, [b, cc], [scm])
                bo = [bank(), bank()]
                for hh in range(2):
                    for h in range(4):
                        bb_ = bo[h // 2]
                        oreg = bb_.ap[hh * 64:(hh + 1) * 64, (h % 2) * 256:(h % 2 + 1) * 256]
                        mm(oreg, scm.ap[:, h * 128 + hh * 64:h * 128 + (hh + 1) * 64], vbf.ap[:, s, h * 256:(h + 1) * 256],
                           True, False, [scm, vbf], [bb_])
                        mm(oreg, qin[h].ap[:, s * 128 + hh * 64:s * 128 + (hh + 1) * 64], Gstb[l].ap[:, h * 256:(h + 1) * 256],
                           False, True, [qin[h], Gstb[l]], [bb_])
                    bs = [bank(), bank()]
                    for h in range(4):
                        bb_ = bs[h // 2]
                        mm(bb_.ap[:, (h % 2) * 256:(h % 2 + 1) * 256], kend.ap[hh * 64:(hh + 1) * 64, h * 128:(h + 1) * 128],
                           vbf.ap[hh * 64:(hh + 1) * 64, s, h * 256:(h + 1) * 256], True, True, [kend, vbf], [bb_])
                    for h in range(4):
                        bb_ = bs[h // 2]
                        ci = 2 * s + hh
                        stt(Gst[l].ap[:, h * 256:(h + 1) * 256], Gst[l].ap[:, h * 256:(h + 1) * 256], egl.ap[:, h, ci:ci + 1],
                            bb_.ap[:, (h % 2) * 256:(h % 2 + 1) * 256], ALU.mult, ALU.add, [Gst[l], egl, bb_], [Gst[l]])
                    acopy(Gstb[l].ap, Gst[l].ap, [Gst[l]], [Gstb[l]])
                for hb in range(2):
                    vcopy(ov.ap[:, hb * 512:(hb + 1) * 512], bo[hb].ap, [bo[hb]], [ov])
                memset(ss4.ap, 0.0, [ss4])
                for h in range(4):
                    act(junk.ap, ov.ap[:, h * 256:(h + 1) * 256], AF.Square, [ov], [junk, ss4], accum=ss4.ap[:, h:h + 1])
                ts(ss4.ap, ss4.ap, 1.0 / 256, EPS, ALU.mult, ALU.add, [ss4], [ss4])
                act(ss4.ap, ss4.ap, AF.Ln, [ss4], [ss4])
                act(ss4.ap, ss4.ap, AF.Exp, [ss4], [ss4], scale=-0.5)
                v4 = lambda a: a.rearrange("p (h q) -> p h q", h=4)
                tt(v4(ov.ap), v4(ov.ap), ss4.ap.unsqueeze(2).to_broadcast([128, 4, 256]), ALU.mult, [ov, ss4], [ov])
                tt(v4(ov.ap), v4(ov.ap), gnw.unsqueeze(1).to_broadcast([128, 4, 256]), ALU.mult, [ov, pp[l]], [ov])
                tt(onb.ap, ov.ap, sgt.ap[:, s, :], ALU.mult, [ov, sgt], [onb])
                for hb in range(2):
                    b = bank()
                    for c4 in range(4):
                        c = hb * 4 + c4
                        mm(b.ap[:, c4 * 128:(c4 + 1) * 128], onb.ap[:, c * 128:(c + 1) * 128], identb.ap, True, True, [onb, identb], [b])
                    for c4 in range(4):
                        c = hb * 4 + c4
                        acopy(oT[c].ap[:, tsl], b.ap[:, c4 * 128:(c4 + 1) * 128], [b], [oT[c]])
            P.barrier()
            AR.off = mark
            outproj_gate(w_og, oT, 2, False)

            P.barrier()
            AR.reset()
            mb = [AR.alloc([T], BF16, "mb%d" % k) for k in range(8)]
            for k in range(8):
                acopy(mb[k].ap, mg[k].ap, [mg[k]], [mb[k]])
            for pc in range(2):
                sl, v = wload([(wsq_cols(w_o, l, pc * 512, 512), lambda v: v)], lambda a: a.rearrange("p (k c) -> p k c", k=8))
                for j4 in range(4):
                    j = pc * 4 + j4
                    b = bank()
                    for k in range(8):
                        mm(b.ap[:, 0:T], v[:, k, j4 * 128:(j4 + 1) * 128], mb[k].ap, k == 0, k == 7, [sl, mb[k]], [b])
                    tt(xT[j].ap, xT[j].ap, b.ap[:, 0:T], ALU.add, [xT[j], b], [xT[j]])

            P.barrier()
            rs = rmsnorm_to_hT(None, None)
            apply_norm(rs, PPs(l, "nw_mlp"), pp[l], hT)
            P.barrier()
            AR.reset()
            hid = [AR.alloc([T], BF16, "hid%d" % f) for f in range(32)]
            rl = [AR.alloc([T], F32, "rl%d" % i) for i in range(2)]
            for pc in range(8):
                sl, v = wload([(wsq_cols(w_up, l, pc * 512, 512), lambda v: v)], lambda a: a.rearrange("p (k c) -> p k c", k=8))
                for f4 in range(4):
                    f = pc * 4 + f4
                    b = bank()
                    for k in range(8):
                        mm(b.ap[:, 0:T], v[:, k, f4 * 128:(f4 + 1) * 128], hT[k].ap, k == 0, k == 7, [sl, hT[k]], [b])
                    r_ = rl[f % 2]
                    act(r_.ap, b.ap[:, 0:T], AF.Relu, [b], [r_])
                    tt(hid[f].ap, r_.ap, r_.ap, ALU.mult, [r_], [hid[f]])
            for j in range(8):
                src = w_dn[l].rearrange("(f p) c -> p f c", p=128)[:, :, j * 128:(j + 1) * 128]
                sl, v = wload([(src, lambda v: v)], lambda a: a.rearrange("p (f c) -> p f c", f=32))
                b = bank()
                for f in range(32):
                    mm(b.ap[:, 0:T], v[:, f, :], hid[f].ap, f == 0, f == 31, [sl, hid[f]], [b])
                tt(xT[j].ap, xT[j].ap, b.ap[:, 0:T], ALU.add, [xT[j], b], [xT[j]])

        P.barrier()
        rs = rmsnorm_to_hT(None, None)
        yf = [AR.alloc([T], F32, "yf%d" % i) for i in range(2)]
        xo = AR.alloc([NS, D], F32, "xo")
        for k in range(8):
            y_ = yf[k % 2]
            stt(y_.ap, xT[k].ap, nwf[:, k:k + 1], rs.ap, ALU.mult, ALU.mult, [xT[k], cc, rs], [y_])
            b = bank()
            for s in range(NS):
                mm(b.ap[:, s * 128:(s + 1) * 128], y_.ap[:, s * 128:(s + 1) * 128], C("ident"), True, True, [y_, cc], [b])
            vcopy(xo.ap[:, :, k * 128:(k + 1) * 128], b.ap[:, 0:T].rearrange("p (s c) -> p s c", s=NS), [b], [xo])
        last_store = P.dma("sp", lambda e, xo=xo, tok0=tok0: e.dma_start(
            out=out_d[tok0:tok0 + T, :].rearrange("(s p) d -> p s d", p=128), in_=xo.ap), [xo], [])
        final_stores.append(last_store)

    P.emit_block(final_stores)
    return nc


def _consts():
    i = np.arange(128)
    ident = (i[:, None] == i[None, :]).astype(np.float32)
    ones = np.ones((128, 128), np.float32)
    tri = (i[:, None] <= i[None, :]).astype(np.float32)
    gt = (i[:, None] > i[None, :]).astype(np.float32)
    same = (i[:, None] // 64) == (i[None, :] // 64)
    blktri = (same & (i[:, None] <= i[None, :])).astype(np.float32)
    blkgt = (same & (i[:, None] > i[None, :])).astype(np.float32)
    return ident, ones, tri, gt, blktri * (-1.0 / 16.0), blkgt * (-1.0 / 16.0), blktri


def _packs(inp, L):
    pk = np.zeros((L, 128, NPAR), np.float32)

    def put(l, name, arr):
        o, w = PP[name]
        pk[l, :, o:o + w] = arr

    for l in range(L):
        put(l, "nw_mix", inp["norm_mix_w"][l].reshape(8, 128).T)
        put(l, "nw_mlp", inp["norm_mlp_w"][l].reshape(8, 128).T)
        put(l, "cwA", inp["conv_a_w"][l].reshape(3, 8, 128).transpose(2, 1, 0).reshape(128, 24))
        put(l, "cwB", inp["ssm_conv_w"][l].reshape(4, 16, 128).transpose(2, 1, 0).reshape(128, 64))
        put(l, "cbB", inp["ssm_conv_b"][l].reshape(16, 128).T)
        put(l, "dtb", np.broadcast_to(inp["ssm_dt_bias"][l][None, :], (128, 16)))
        put(l, "alog", np.broadcast_to(inp["ssm_a_log"][l][None, :], (128, 16)))
        put(l, "dsk", np.broadcast_to(inp["ssm_d"][l][None, :], (128, 16)))
        put(l, "snw", np.broadcast_to(inp["ssm_norm_w"][l][None, :], (128, 1024)))
        put(l, "gnw", np.broadcast_to(inp["gla_norm_w"][l][None, :], (128, 256)))
        o, w = PP["wgk"]
        pk[l, 0:16, o:o + w] = inp["gla_w_gk2"][l]
        pk[l, 16, o:o + w] = inp["gla_b_gk"][l]
    cs = np.zeros((128, NCONST), np.float32)
    for i, c in enumerate(_consts()):
        cs[:, i * 128:(i + 1) * 128] = c
    cs[:, 7 * 128:7 * 128 + 8] = inp["norm_f_w"].reshape(8, 128).T
    return pk, cs


def make_in_maps(inp, n_seq, L):
    pk, cs = _packs(inp, L)
    f = lambda a: np.ascontiguousarray(np.asarray(a, dtype=np.float32))
    shared = {
        "w_in": f(inp["w_in"]), "w_out_a": f(inp["w_out_a"]), "w_out_ssm": f(inp["w_out_ssm"]),
        "w_out_gla": f(inp["w_out_gla"]), "w_o": f(inp["w_o"]), "w_mlp_up": f(inp["w_mlp_up"]),
        "w_mlp_down": f(inp["w_mlp_down"]), "ppack": pk, "cpack": cs,
    }
    maps = []
    for b in range(n_seq):
        m = dict(shared)
        m["x"] = f(inp["x"][b])
        maps.append(m)
    return maps


_NC_CACHE = {}


def kernel(**inputs):
    inp = {k: np.asarray(v) for k, v in inputs.items()}
    B, S, _ = inp["x"].shape
    L = inp["w_in"].shape[0]
    T = 512
    key = (S // T, T, L)
    if key not in _NC_CACHE:
        _NC_CACHE[key] = build(S // T, T, L)
    nc = _NC_CACHE[key]
    maps = make_in_maps(inp, B, L)
    in_maps = [maps[c] for c in range(B)]
    res = run_bass_kernel_spmd(nc, in_maps, core_ids=list(range(B)))
    out = np.stack([np.asarray(res.results[b]["out"], dtype=np.float32) for b in range(B)], axis=0)
    return out
```

```python
import numpy as np
import concourse.bass as bass
import concourse.mybir as mybir
from concourse.bass_utils import run_bass_kernel_spmd

F32 = mybir.dt.float32
BF16 = mybir.dt.bfloat16
AF = mybir.ActivationFunctionType
ALU = mybir.AluOpType

D = 1024
INW = 12320
DFF = 4096
EPS = 1e-6
O_AX, O_AB, O_AC = 0, 1024, 2048
O_Z, O_XBC, O_DT = 3072, 4096, 6144
O_Q, O_K, O_V, O_G, O_GK = 6160, 6672, 7184, 8208, 9232
O_GATE = 9248

PP = {}
_o = 0
for _n, _w in [("nw_mix", 8), ("nw_mlp", 8), ("cwA", 24), ("cwB", 64), ("cbB", 16), ("dtb", 16),
               ("alog", 16), ("dsk", 16), ("snw", 1024), ("gnw", 256), ("wgk", 512)]:
    PP[_n] = (_o, _w)
    _o += _w
NPAR = _o
NCONST = 7 * 128 + 8


class Op:
    __slots__ = ("eng", "fn", "deps", "dma_sem", "token", "signal", "is_dma")

    def __init__(self, eng, fn, deps, is_dma=False):
        self.eng = eng
        self.fn = fn
        self.deps = deps
        self.is_dma = is_dma
        self.dma_sem = None
        self.token = None
        self.signal = False


class Buf:
    __slots__ = ("w", "r", "name", "excl", "wx")

    def __init__(self, name=""):
        self.w = None
        self.r = {}
        self.name = name
        self.wx = []
        self.excl = False


class Tile(Buf):
    __slots__ = ("ap",)

    def __init__(self, ap, name=""):
        Buf.__init__(self, name)
        self.ap = ap


class Prog:
    ENGS = ("pe", "act", "dve", "pool", "sp")

    def __init__(self, nc, same_engine_sync=True):
        self.nc = nc
        self.ops = {e: [] for e in self.ENGS}
        self.last = {e: None for e in self.ENGS}
        self.barrier_deps = []
        self.dmas_since_barrier = []
        self.same_engine_sync = same_engine_sync
        self.n_dma_sems = 0

    def op(self, eng, fn, r=(), w=(), is_dma=False, group_prev=None):
        xr = [b for b in r if b.excl]
        if xr:
            r = [b for b in r if not b.excl]
            w = list(w) + [b for b in xr if b not in w]
        deps = {}
        for d in self.barrier_deps:
            deps[id(d)] = d
        for b in r:
            if b.w is not None:
                deps[id(b.w)] = b.w
            for o in b.wx:
                deps[id(o)] = o
        for b in w:
            if b.w is not None:
                deps[id(b.w)] = b.w
            for o in b.wx:
                deps[id(o)] = o
            for o in b.r.values():
                deps[id(o)] = o
        if group_prev is not None:
            for g_ in group_prev:
                deps.pop(id(g_), None)
            for d_ in group_prev[0].deps:
                deps[id(d_)] = d_
        o = Op(eng, fn, list(deps.values()), is_dma)
        for b in r:
            b.r[eng if not is_dma else ("dma", id(o))] = o
        for b in w:
            if group_prev is not None and b.w is group_prev[-1]:
                b.wx = list(group_prev)
            else:
                b.wx = []
            b.w = o
            b.r = {}
        self.ops[eng].append(o)
        self.last[eng] = o
        if is_dma:
            self.dmas_since_barrier.append(o)
        return o

    def pe(self, fn, r=(), w=()):
        return self.op("pe", fn, r, w)

    def act(self, fn, r=(), w=()):
        return self.op("act", fn, r, w)

    def dve(self, fn, r=(), w=()):
        return self.op("dve", fn, r, w)

    def dma(self, eng, fn, r=(), w=(), group_prev=None):
        return self.op(eng, fn, r, w, is_dma=True, group_prev=group_prev)

    def barrier(self):
        deps = [o for o in self.last.values() if o is not None and not o.is_dma]
        deps += self.dmas_since_barrier
        self.barrier_deps = deps
        self.dmas_since_barrier = []

    def emit_block(self, final_wait_ops):
        nc = self.nc
        for e in self.ENGS:
            for o in self.ops[e]:
                for d in o.deps:
                    if d.is_dma or d.eng != o.eng or (d.eng != "pe" and self.same_engine_sync):
                        d.signal = True
        for o in final_wait_ops:
            o.signal = True
        from contextlib import ExitStack
        with ExitStack() as st:
            esem = {e: st.enter_context(nc.semaphore("se_" + e)) for e in self.ENGS}
            NDS = 48
            dsem = [st.enter_context(nc.semaphore("sd%d" % i)) for i in range(NDS)]
            for e in self.ENGS:
                cnt = 0
                for o in self.ops[e]:
                    if o.is_dma:
                        continue
                    if o.signal:
                        cnt += 1
                        o.token = (esem[e], cnt)
            dcount = [0] * NDS
            dlast = [None] * NDS
            di = 0
            order = []
            for e in self.ENGS:
                for o in self.ops[e]:
                    if o.is_dma:
                        order.append(o)
            per_eng = {"sp": list(range(0, 16)), "pool": list(range(16, 48)), "act": [], "pe": [], "dve": []}
            rr = {e: 0 for e in self.ENGS}
            for e in self.ENGS:
                for o in self.ops[e]:
                    if not o.is_dma:
                        continue
                    lst = per_eng[e]
                    s = lst[rr[e] % len(lst)]
                    rr[e] += 1
                    dcount[s] += 1
                    o.token = (dsem[s], 16 * dcount[s])
                    if dlast[s] is not None:
                        o.deps.append(dlast[s])
                    dlast[s] = o
            blk = st.enter_context(nc.Block())

            def run(eng_name, eng):
                waited = {}
                for o in self.ops[eng_name]:
                    for d in o.deps:
                        if not d.is_dma and d.eng == eng_name and (eng_name == "pe" or not self.same_engine_sync):
                            continue
                        sem, val = d.token
                        k = id(sem)
                        if waited.get(k, 0) >= val:
                            continue
                        waited[k] = val
                        eng.wait_ge(sem, val)
                    ins = o.fn(eng)
                    if o.is_dma or o.signal:
                        sem, val = o.token
                        ins.then_inc(sem, 16 if o.is_dma else 1)
                if eng_name == "sp":
                    for o in final_wait_ops:
                        sem, val = o.token
                        eng.wait_ge(sem, val)

            @blk.tensor
            def _(e):
                run("pe", e)

            @blk.scalar
            def _(e):
                run("act", e)

            @blk.vector
            def _(e):
                run("dve", e)

            @blk.gpsimd
            def _(e):
                run("pool", e)

            @blk.sync
            def _(e):
                run("sp", e)


class Arena:
    def __init__(self, nc, name, nfloats):
        self.t = nc.alloc_sbuf_tensor(name, [128, nfloats], F32)
        self.n = nfloats
        self.off = 0

    def reset(self):
        self.off = 0

    def alloc(self, free_shape, dtype=F32, name=""):
        n = int(np.prod(free_shape))
        nf = n if dtype == F32 else (n + 1) // 2
        assert self.off + nf <= self.n, ("arena overflow", name, self.off, nf, self.n)
        a = self.t[:, self.off:self.off + nf]
        self.off += nf
        if dtype != F32:
            a = a.bitcast(dtype)
            if n % 2:
                a = a[:, 0:n]
        if len(free_shape) == 2:
            a = a.rearrange("p (a b) -> p a b", a=free_shape[0])
        elif len(free_shape) == 3:
            a = a.rearrange("p (a b c) -> p a b c", a=free_shape[0], b=free_shape[1])
        return Tile(a, name)


def build(NTILE, T, L=2):
    assert T % 128 == 0 and T <= 512
    NS = T // 128
    S = NTILE * T
    nc = bass.Bass("TRN2", target_bir_lowering=False)
    x_d = nc.dram_tensor("x", [S, D], F32, kind="ExternalInput").ap()
    out_d = nc.dram_tensor("out", [S, D], F32, kind="ExternalOutput").ap()
    w_in = nc.dram_tensor("w_in", [L, D, INW], F32, kind="ExternalInput").ap()
    w_oa = nc.dram_tensor("w_out_a", [L, D, D], F32, kind="ExternalInput").ap()
    w_os = nc.dram_tensor("w_out_ssm", [L, D, D], F32, kind="ExternalInput").ap()
    w_og = nc.dram_tensor("w_out_gla", [L, D, D], F32, kind="ExternalInput").ap()
    w_o = nc.dram_tensor("w_o", [L, D, D], F32, kind="ExternalInput").ap()
    w_up = nc.dram_tensor("w_mlp_up", [L, D, DFF], F32, kind="ExternalInput").ap()
    w_dn = nc.dram_tensor("w_mlp_down", [L, DFF, D], F32, kind="ExternalInput").ap()
    pp_d = nc.dram_tensor("ppack", [L, 128, NPAR], F32, kind="ExternalInput").ap()
    cc_d = nc.dram_tensor("cpack", [128, NCONST], F32, kind="ExternalInput").ap()

    P = Prog(nc)
    final_stores = []

    def sb(name, shape, dt=F32):
        return Tile(nc.alloc_sbuf_tensor(name, shape, dt)[:], name)

    xT = [sb("xT%d" % k, [128, T]) for k in range(8)]
    hT = [sb("hT%d" % k, [128, T], BF16) for k in range(8)]
    mg = [sb("mg%d" % k, [128, T]) for k in range(8)]
    NWS = 4
    wslot = [sb("wsl%d" % i, [128, 4096], BF16) for i in range(NWS)]
    wsmall = [sb("wsm%d" % i, [128, 8, 16], BF16) for i in range(2)]
    pp = [sb("pp%d" % l, [128, NPAR]) for l in range(L)]
    cc = sb("cc", [128, NCONST])
    identb = sb("identb", [128, 128], BF16)
    negA = [sb("negA%d" % l, [128, 16]) for l in range(L)]
    Sst = [sb("Sst%d" % l, [128, 1024]) for l in range(L)]
    Sstb = [sb("Sstb%d" % l, [128, 1024], BF16) for l in range(L)]
    Gst = [sb("Gst%d" % l, [128, 1024]) for l in range(L)]
    Gstb = [sb("Gstb%d" % l, [128, 1024], BF16) for l in range(L)]
    haloA = [sb("haloA%d" % l, [128, 8, 2]) for l in range(L)]
    haloB = [sb("haloB%d" % l, [128, 16, 3]) for l in range(L)]
    AR = Arena(nc, "arena", 20000)
    psb = [Tile(nc.alloc_psum_tensor("ps%d" % i, [128, 512], F32)[:], "ps%d" % i) for i in range(8)]
    pcnt = [0]

    def bank():
        b = psb[pcnt[0] % 8]
        pcnt[0] += 1
        return b

    def C(name):
        i = ["ident", "ones", "tri", "gt", "blktri", "blkgt", "maskg"].index(name)
        return cc.ap[:, i * 128:(i + 1) * 128]

    nwf = cc.ap[:, 7 * 128:7 * 128 + 8]

    def PPs(l, name):
        o, w = PP[name]
        return pp[l].ap[:, o:o + w]

    def mm(out, lhsT, rhs, start, stop, r, w):
        return P.pe(lambda e: e.matmul(out, lhsT, rhs, start=start, stop=stop), r, w)

    def act(out, in_, func, r, w, bias=None, scale=None, accum=None):
        kw = {}
        if bias is not None:
            kw["bias"] = bias
        if scale is not None:
            kw["scale"] = scale
        if accum is not None:
            kw["accum_out"] = accum
        return P.act(lambda e: e.activation(out=out, in_=in_, func=func, **kw), r, w)

    def tt(out, a, b, op, r, w):
        return P.dve(lambda e: e.tensor_tensor(out=out, in0=a, in1=b, op=op), r, w)

    def stt(out, in0, scalar, in1, op0, op1, r, w):
        return P.dve(lambda e: e.scalar_tensor_tensor(out=out, in0=in0, scalar=scalar, in1=in1, op0=op0, op1=op1), r, w)

    def ts(out, in0, s1, s2, op0, op1, r, w):
        if s2 is None:
            return P.dve(lambda e: e.tensor_scalar(out=out, in0=in0, scalar1=s1, scalar2=None, op0=op0), r, w)
        return P.dve(lambda e: e.tensor_scalar(out=out, in0=in0, scalar1=s1, scalar2=s2, op0=op0, op1=op1), r, w)

    def vcopy(out, in_, r, w):
        return P.dve(lambda e: e.tensor_copy(out=out, in_=in_), r, w)

    def acopy(out, in_, r, w):
        return P.act(lambda e: e.activation(out=out, in_=in_, func=AF.Copy), r, w)

    def memset(ap, val, w):
        return P.dve(lambda e: e.memset(ap, val), (), w)

    wctr = [0]

    def wload(src_aps, view):
        sl = wslot[wctr[0] % NWS]
        wctr[0] += 1
        v = view(sl.ap)
        grp = []
        for (src, dst) in src_aps:
            d = dst(v)
            o_ = P.dma("pool", lambda e, d=d, src=src: e.dma_start(out=d, in_=src), (), [sl],
                       group_prev=(list(grp) if grp else None))
            grp.append(o_)
        return sl, v

    def win_cols(l, c0, n):
        return w_in[l].rearrange("(k p) c -> p k c", p=128)[:, :, c0:c0 + n]

    def wsq_cols(wd, l, c0, n):
        return wd[l].rearrange("(k p) c -> p k c", p=128)[:, :, c0:c0 + n]

    smctr = [0]

    def wload_small(l, c0):
        sl = wsmall[smctr[0] % 2]
        smctr[0] += 1
        src = win_cols(l, c0, 16)
        P.dma("pool", lambda e: e.dma_start(out=sl.ap, in_=src), (), [sl])
        return sl

    P.dma("sp", lambda e: e.dma_start(out=cc.ap, in_=cc_d), (), [cc])
    for l in range(L):
        P.dma("sp", lambda e, l=l: e.dma_start(out=pp[l].ap, in_=pp_d[l]), (), [pp[l]])
    vcopy(identb.ap, C("ident"), [cc], [identb])
    for l in range(L):
        memset(Sst[l].ap, 0.0, [Sst[l]])
        memset(Sstb[l].ap, 0.0, [Sstb[l]])
        memset(Gst[l].ap, 0.0, [Gst[l]])
        memset(Gstb[l].ap, 0.0, [Gstb[l]])
        memset(haloA[l].ap, 0.0, [haloA[l]])
        memset(haloB[l].ap, 0.0, [haloB[l]])
        act(negA[l].ap, PPs(l, "alog"), AF.Exp, [pp[l]], [negA[l]])
        ts(negA[l].ap, negA[l].ap, -1.0, None, ALU.mult, None, [negA[l]], [negA[l]])

    def rmsnorm_to_hT(nw_ap, nw_buf):
        AR.reset()
        sq = [AR.alloc([T], F32, "sq%d" % i) for i in range(2)]
        rs = AR.alloc([T], F32, "rstd")
        b = bank()
        for k in range(8):
            s = sq[k % 2]
            act(s.ap, xT[k].ap, AF.Square, [xT[k]], [s])
            mm(b.ap[:, 0:T], C("ones"), s.ap, k == 0, k == 7, [cc, s], [b])
        ts(rs.ap, b.ap[:, 0:T], 1.0 / D, EPS, ALU.mult, ALU.add, [b], [rs])
        act(rs.ap, rs.ap, AF.Ln, [rs], [rs])
        act(rs.ap, rs.ap, AF.Exp, [rs], [rs], scale=-0.5)
        return rs

    def apply_norm(rs, nw_ap, nw_buf, outs):
        for k in range(8):
            stt(outs[k].ap, xT[k].ap, nw_ap[:, k:k + 1], rs.ap, ALU.mult, ALU.mult, [xT[k], nw_buf, rs], [outs[k]])

    for ti in range(NTILE):
        tok0 = ti * T
        P.barrier()
        AR.reset()
        xin = AR.alloc([NS, D], F32, "xin")
        P.dma("sp", lambda e, xin=xin, tok0=tok0: e.dma_start(
            out=xin.ap, in_=x_d[tok0:tok0 + T, :].rearrange("(s p) d -> p s d", p=128)), (), [xin])
        for k in range(8):
            b = bank()
            for s in range(NS):
                mm(b.ap[:, s * 128:(s + 1) * 128], xin.ap[:, s, k * 128:(k + 1) * 128], C("ident"), True, True, [xin, cc], [b])
            vcopy(xT[k].ap, b.ap[:, 0:T], [b], [xT[k]])

        for l in range(L):
            P.barrier()
            rs = rmsnorm_to_hT(None, None)
            apply_norm(rs, PPs(l, "nw_mix"), pp[l], hT)

            P.barrier()
            AR.reset()
            ub = [AR.alloc([T], BF16, "ub%d" % c) for c in range(8)]
            pbuf = [AR.alloc([T + 2], F32, "p%d" % i) for i in range(2)]
            axs = [AR.alloc([T], F32, "axs%d" % i) for i in range(2)]
            u0 = [AR.alloc([T], F32, "u0%d" % i) for i in range(2)]
            u1 = [AR.alloc([T], F32, "u1%d" % i) for i in range(2)]
            cw = PPs(l, "cwA")
            for c in range(8):
                sl, v = wload(
                    [(win_cols(l, O_AX + c * 128, 128), lambda v: v[:, :, 0:128]),
                     (win_cols(l, O_AB + c * 128, 128), lambda v: v[:, :, 128:256]),
                     (win_cols(l, O_AC + c * 128, 128), lambda v: v[:, :, 256:384])],
                    lambda a: a[:, 0:8 * 384].rearrange("p (k c) -> p k c", k=8))
                bx, bb, bc = bank(), bank(), bank()
                for (bk, co) in ((bx, 0), (bb, 128), (bc, 256)):
                    for k in range(8):
                        mm(bk.ap[:, 0:T], v[:, k, co:co + 128], hT[k].ap, k == 0, k == 7, [sl, hT[k]], [bk])
                pb = pbuf[c % 2]
                ax = axs[c % 2]
                a0 = u0[c % 2]
                a1 = u1[c % 2]
                acopy(ax.ap, bx.ap[:, 0:T], [bx], [ax])
                vcopy(pb.ap[:, 0:2], haloA[l].ap[:, c, :], [haloA[l]], [pb])
                tt(pb.ap[:, 2:2 + T], bc.ap[:, 0:T], ax.ap, ALU.mult, [bc, ax], [pb])
                vcopy(haloA[l].ap[:, c, :], pb.ap[:, T:T + 2], [pb], [haloA[l]])
                act(a0.ap, pb.ap[:, 0:T], AF.Identity, [pb, pp[l]], [a0], scale=cw[:, c * 3:c * 3 + 1])
                stt(a1.ap, pb.ap[:, 1:1 + T], cw[:, c * 3 + 1:c * 3 + 2], a0.ap, ALU.mult, ALU.add, [pb, a0, pp[l]], [a1])
                stt(a0.ap, pb.ap[:, 2:2 + T], cw[:, c * 3 + 2:c * 3 + 3], a1.ap, ALU.mult, ALU.add, [pb, a1, pp[l]], [a0])
                tt(ub[c].ap, a0.ap, bb.ap[:, 0:T], ALU.mult, [a0, bb], [ub[c]])

            def outproj_gate(wd, act_in, br, first):
                sg = [AR.alloc([T], F32, "sg%d" % i) for i in range(2)]
                tm = [AR.alloc([T], F32, "tm%d" % i) for i in range(2)]
                for j in range(8):
                    sl, v = wload(
                        [(wsq_cols(wd, l, j * 128, 128), lambda v: v[:, :, 0:128]),
                         (win_cols(l, O_GATE + br * 1024 + j * 128, 128), lambda v: v[:, :, 128:256])],
                        lambda a: a[:, 0:8 * 256].rearrange("p (k c) -> p k c", k=8))
                    by, bg = bank(), bank()
                    for k in range(8):
                        mm(by.ap[:, 0:T], v[:, k, 0:128], act_in[k].ap, k == 0, k == 7, [sl, act_in[k]], [by])
                    for k in range(8):
                        mm(bg.ap[:, 0:T], v[:, k, 128:256], hT[k].ap, k == 0, k == 7, [sl, hT[k]], [bg])
                    s = sg[j % 2]
                    act(s.ap, bg.ap[:, 0:T], AF.Sigmoid, [bg], [s])
                    if first:
                        tt(mg[j].ap, s.ap, by.ap[:, 0:T], ALU.mult, [s, by], [mg[j]])
                    else:
                        t_ = tm[j % 2]
                        tt(t_.ap, s.ap, by.ap[:, 0:T], ALU.mult, [s, by], [t_])
                        tt(mg[j].ap, mg[j].ap, t_.ap, ALU.add, [mg[j], t_], [mg[j]])

            outproj_gate(w_oa, ub, 0, True)

            P.barrier()
            AR.reset()
            xs_fm = [AR.alloc([T], BF16, "xsfm%d" % c) for c in range(8)]
            BmT = [AR.alloc([T], BF16, "BmT%d" % g) for g in range(4)]
            CmT = [AR.alloc([T], BF16, "CmT%d" % g) for g in range(4)]
            sz = AR.alloc([NS, 1024], F32, "sz")
            yT = [AR.alloc([T], BF16, "yT%d" % c) for c in range(8)]
            dtr = AR.alloc([NS, 16], F32, "dtraw")
            mark = AR.off
            xraw = [AR.alloc([T + 3], F32, "xraw%d" % i) for i in range(2)]
            c0 = [AR.alloc([T], F32, "c0%d" % i) for i in range(2)]
            c1 = [AR.alloc([T], F32, "c1%d" % i) for i in range(2)]
            cwB = PPs(l, "cwB")
            cbB = PPs(l, "cbB")
            for pc in range(4):
                sl, v = wload([(win_cols(l, O_XBC + pc * 512, 512), lambda v: v)],
                              lambda a: a.rearrange("p (k c) -> p k c", k=8))
                for cc_ in range(4):
                    c = pc * 4 + cc_
                    b = bank()
                    for k in range(8):
                        mm(b.ap[:, 0:T], v[:, k, cc_ * 128:(cc_ + 1) * 128], hT[k].ap, k == 0, k == 7, [sl, hT[k]], [b])
                    xr = xraw[c % 2]
                    a0 = c0[c % 2]
                    a1 = c1[c % 2]
                    vcopy(xr.ap[:, 0:3], haloB[l].ap[:, c, :], [haloB[l]], [xr])
                    acopy(xr.ap[:, 3:3 + T], b.ap[:, 0:T], [b], [xr])
                    vcopy(haloB[l].ap[:, c, :], xr.ap[:, T:T + 3], [xr], [haloB[l]])
                    act(a0.ap, xr.ap[:, 0:T], AF.Identity, [xr, pp[l]], [a0], scale=cwB[:, c * 4:c * 4 + 1], bias=cbB[:, c:c + 1])
                    stt(a1.ap, xr.ap[:, 1:1 + T], cwB[:, c * 4 + 1:c * 4 + 2], a0.ap, ALU.mult, ALU.add, [xr, a0, pp[l]], [a1])
                    stt(a0.ap, xr.ap[:, 2:2 + T], cwB[:, c * 4 + 2:c * 4 + 3], a1.ap, ALU.mult, ALU.add, [xr, a1, pp[l]], [a0])
                    stt(a1.ap, xr.ap[:, 3:3 + T], cwB[:, c * 4 + 3:c * 4 + 4], a0.ap, ALU.mult, ALU.add, [xr, a0, pp[l]], [a1])
                    dst = xs_fm[c] if c < 8 else (BmT[c - 8] if c < 12 else CmT[c - 12])
                    act(dst.ap, a1.ap, AF.Silu, [a1], [dst])
            for pc in range(2):
                sl, v = wload([(win_cols(l, O_Z + pc * 512, 512), lambda v: v)],
                              lambda a: a.rearrange("p (k c) -> p k c", k=8))
                for s in range(NS):
                    b = bank()
                    for k in range(8):
                        mm(b.ap, hT[k].ap[:, s * 128:(s + 1) * 128], v[:, k, :], k == 0, k == 7, [sl, hT[k]], [b])
                    act(sz.ap[:, s, pc * 512:(pc + 1) * 512], b.ap, AF.Silu, [b], [sz])
            sm = wload_small(l, O_DT)
            for s in range(NS):
                b = bank()
                for k in range(8):
                    mm(b.ap[:, 0:16], hT[k].ap[:, s * 128:(s + 1) * 128], sm.ap[:, k, :], k == 0, k == 7, [sm, hT[k]], [b])
                tt(dtr.ap[:, s, :], b.ap[:, 0:16], PPs(l, "dtb"), ALU.add, [b, pp[l]], [dtr])
            act(dtr.ap, dtr.ap, AF.Exp, [dtr], [dtr])
            act(dtr.ap, dtr.ap, AF.Ln, [dtr], [dtr], bias=1.0)

            P.barrier()
            AR.off = mark
            dtA = AR.alloc([16], F32, "dtA")
            acs = AR.alloc([32], F32, "acs")
            eac = AR.alloc([16], F32, "eac")
            cd = AR.alloc([16], F32, "cd")
            dte = AR.alloc([16], F32, "dte")
            xdt = AR.alloc([1024], BF16, "xdt")
            xstm = AR.alloc([1024], F32, "xstm")
            xdte = AR.alloc([1024], BF16, "xdte")
            Bmtm = AR.alloc([512], BF16, "Bmtm")
            CBTm = AR.alloc([512], F32, "CBTm")
            Lh = [AR.alloc([128], F32, "Lh%d" % i) for i in range(4)]
            Ee = [AR.alloc([512], F32, "Ee%d" % i) for i in range(2)]
            MT = [AR.alloc([512], BF16, "MT%d" % g) for g in range(4)]
            yv = AR.alloc([1024], F32, "yv")
            yv2 = AR.alloc([1024], F32, "yv2")
            ynb = AR.alloc([1024], BF16, "ynb")
            ss4 = AR.alloc([4], F32, "ss4")
            junk = AR.alloc([256], F32, "junk")
            stmp = AR.alloc([1024], F32, "stmp")
            dsk = PPs(l, "dsk")
            snw = PPs(l, "snw")
            for s in range(NS):
                tsl = slice(s * 128, (s + 1) * 128)
                tt(dtA.ap, dtr.ap[:, s, :], negA[l].ap, ALU.mult, [dtr, negA[l]], [dtA])
                bA = bank()
                mm(bA.ap[:, 0:16], C("tri"), dtA.ap, True, True, [cc, dtA], [bA])
                mm(bA.ap[:, 16:32], C("ones"), dtA.ap, True, True, [cc, dtA], [bA])
                vcopy(acs.ap, bA.ap[:, 0:32], [bA], [acs])
                act(eac.ap, acs.ap[:, 0:16], AF.Exp, [acs], [eac])
                act(cd.ap, acs.ap[:, 16:32], AF.Exp, [acs], [cd])
                tt(dte.ap, acs.ap[:, 16:32], acs.ap[:, 0:16], ALU.subtract, [acs], [dte])
                act(dte.ap, dte.ap, AF.Exp, [dte], [dte])
                for hb in range(2):
                    b = bank()
                    for c4 in range(4):
                        c = hb * 4 + c4
                        mm(b.ap[:, c4 * 128:(c4 + 1) * 128], xs_fm[c].ap[:, tsl], identb.ap, True, True, [xs_fm[c], identb], [b])
                    hs = slice(hb * 512, (hb + 1) * 512)
                    vcopy(xstm.ap[:, hs], b.ap, [b], [xstm])
                    tt(xdt.ap[:, hs].rearrange("p (h q) -> p h q", q=64), b.ap.rearrange("p (h q) -> p h q", q=64),
                       dtr.ap[:, s, hb * 8:(hb + 1) * 8].unsqueeze(2).to_broadcast([128, 8, 64]), ALU.mult, [b, dtr], [xdt])
                tt(xdte.ap.rearrange("p (h q) -> p h q", q=64), xdt.ap.rearrange("p (h q) -> p h q", q=64),
                   dte.ap.unsqueeze(2).to_broadcast([128, 16, 64]), ALU.mult, [xdt, dte], [xdte])
                b = bank()
                for g in range(4):
                    mm(b.ap[:, g * 128:(g + 1) * 128], BmT[g].ap[:, tsl], identb.ap, True, True, [BmT[g], identb], [b])
                acopy(Bmtm.ap, b.ap, [b], [Bmtm])
                b = bank()
                for g in range(4):
                    mm(b.ap[:, g * 128:(g + 1) * 128], BmT[g].ap[:, tsl], CmT[g].ap[:, tsl], True, True, [BmT[g], CmT[g]], [b])
                tt(CBTm.ap.rearrange("p (g l) -> p g l", g=4), b.ap.rearrange("p (g l) -> p g l", g=4),
                   C("tri").unsqueeze(1).to_broadcast([128, 4, 128]), ALU.mult, [b, cc], [CBTm])
                for g in range(4):
                    b = bank()
                    for r_ in range(4):
                        h = g * 4 + r_
                        lh = Lh[r_]
                        ts(lh.ap, C("gt"), dtA.ap[:, h:h + 1], None, ALU.mult, None, [cc, dtA], [lh])
                        mm(b.ap[:, r_ * 128:(r_ + 1) * 128], lh.ap, C("tri"), True, True, [lh, cc], [b])
                    ee = Ee[g % 2]
                    act(ee.ap, b.ap, AF.Exp, [b], [ee])
                    tt(MT[g].ap.rearrange("p (r l) -> p r l", r=4), ee.ap.rearrange("p (r l) -> p r l", r=4),
                       CBTm.ap[:, g * 128:(g + 1) * 128].unsqueeze(1).to_broadcast([128, 4, 128]), ALU.mult, [ee, CBTm], [MT[g]])
                bd = [bank(), bank()]
                bo = [bank(), bank()]
                for h in range(16):
                    g, r_ = h // 4, h % 4
                    bb_ = bd[h // 8]
                    mm(bb_.ap[:, (h % 8) * 64:(h % 8 + 1) * 64], MT[g].ap[:, r_ * 128:(r_ + 1) * 128], xdt.ap[:, h * 64:(h + 1) * 64],
                       True, True, [MT[g], xdt], [bb_])
                for g in range(4):
                    bb_ = bo[g // 2]
                    mm(bb_.ap[:, (g % 2) * 256:(g % 2 + 1) * 256], CmT[g].ap[:, tsl], Sstb[l].ap[:, g * 256:(g + 1) * 256],
                       True, True, [CmT[g], Sstb[l]], [bb_])
                for hb in range(2):
                    hs = slice(hb * 512, (hb + 1) * 512)
                    v3 = lambda a: a.rearrange("p (h q) -> p h q", q=64)
                    tt(v3(yv.ap[:, hs]), v3(bo[hb].ap), eac.ap[:, hb * 8:(hb + 1) * 8].unsqueeze(2).to_broadcast([128, 8, 64]),
                       ALU.mult, [bo[hb], eac], [yv])
                    tt(yv.ap[:, hs], yv.ap[:, hs], bd[hb].ap, ALU.add, [yv, bd[hb]], [yv])
                    tt(v3(yv2.ap[:, hs]), v3(xstm.ap[:, hs]), dsk[:, hb * 8:(hb + 1) * 8].unsqueeze(2).to_broadcast([128, 8, 64]),
                       ALU.mult, [xstm, pp[l]], [yv2])
                    tt(yv.ap[:, hs], yv.ap[:, hs], yv2.ap[:, hs], ALU.add, [yv, yv2], [yv])
                    tt(yv.ap[:, hs], yv.ap[:, hs], sz.ap[:, s, hs], ALU.mult, [yv, sz], [yv])
                bs = [bank(), bank()]
                for g in range(4):
                    bb_ = bs[g // 2]
                    mm(bb_.ap[:, (g % 2) * 256:(g % 2 + 1) * 256], Bmtm.ap[:, g * 128:(g + 1) * 128], xdte.ap[:, g * 256:(g + 1) * 256],
                       True, True, [Bmtm, xdte], [bb_])
                for hb in range(2):
                    hs = slice(hb * 512, (hb + 1) * 512)
                    v3 = lambda a: a.rearrange("p (h q) -> p h q", q=64)
                    tt(v3(stmp.ap[:, hs]), v3(Sst[l].ap[:, hs]), cd.ap[:, hb * 8:(hb + 1) * 8].unsqueeze(2).to_broadcast([128, 8, 64]),
                       ALU.mult, [Sst[l], cd], [stmp])
                    tt(Sst[l].ap[:, hs], stmp.ap[:, hs], bs[hb].ap, ALU.add, [stmp, bs[hb]], [Sst[l]])
                acopy(Sstb[l].ap, Sst[l].ap, [Sst[l]], [Sstb[l]])
                memset(ss4.ap, 0.0, [ss4])
                for g in range(4):
                    act(junk.ap, yv.ap[:, g * 256:(g + 1) * 256], AF.Square, [yv], [junk, ss4], accum=ss4.ap[:, g:g + 1])
                ts(ss4.ap, ss4.ap, 1.0 / 256, EPS, ALU.mult, ALU.add, [ss4], [ss4])
                act(ss4.ap, ss4.ap, AF.Ln, [ss4], [ss4])
                act(ss4.ap, ss4.ap, AF.Exp, [ss4], [ss4], scale=-0.5)
                tt(yv.ap.rearrange("p (g q) -> p g q", g=4), yv.ap.rearrange("p (g q) -> p g q", g=4),
                   ss4.ap.unsqueeze(2).to_broadcast([128, 4, 256]), ALU.mult, [yv, ss4], [yv])
                tt(ynb.ap, yv.ap, snw, ALU.mult, [yv, pp[l]], [ynb])
                for hb in range(2):
                    b = bank()
                    for c4 in range(4):
                        c = hb * 4 + c4
                        mm(b.ap[:, c4 * 128:(c4 + 1) * 128], ynb.ap[:, c * 128:(c + 1) * 128], identb.ap, True, True, [ynb, identb], [b])
                    for c4 in range(4):
                        c = hb * 4 + c4
                        acopy(yT[c].ap[:, tsl], b.ap[:, c4 * 128:(c4 + 1) * 128], [b], [yT[c]])
            P.barrier()
            AR.off = mark
            outproj_gate(w_os, yT, 1, False)

            P.barrier()
            AR.reset()
            qin = [AR.alloc([T], BF16, "qin%d" % h) for h in range(4)]
            kin = [AR.alloc([T], BF16, "kin%d" % h) for h in range(4)]
            egl = AR.alloc([4, T // 64], F32, "egl")
            gkl = AR.alloc([T], F32, "gkl")
            gkpos = AR.alloc([NS, 512], F32, "gkpos")
            ktm = AR.alloc([NS, 512], F32, "ktm")
            vbf = AR.alloc([NS, 1024], BF16, "vbf")
            sgt = AR.alloc([NS, 1024], F32, "sgt")
            oT = [AR.alloc([T], BF16, "oT%d" % c) for c in range(8)]
            mark = AR.off
            sm = wload_small(l, O_GK)
            memset(gkl.ap[0:32, :], 1.0, [gkl])
            b = bank()
            for k in range(8):
                mm(b.ap[0:16, 0:T], sm.ap[:, k, :], hT[k].ap, k == 0, k == 7, [sm, hT[k]], [b])
            vcopy(gkl.ap[0:16, :], b.ap[0:16, 0:T], [b], [gkl])
            wgk = PPs(l, "wgk")
            for s in range(NS):
                b = bank()
                mm(b.ap, gkl.ap[0:17, s * 128:(s + 1) * 128], wgk[0:17, :], True, True, [gkl, pp[l]], [b])
                act(gkpos.ap[:, s, :], b.ap, AF.Exp, [b], [gkpos], scale=-1.0)
            act(gkpos.ap, gkpos.ap, AF.Ln, [gkpos], [gkpos], bias=1.0)
            eq = [AR.alloc([T], F32, "eq%d" % i) for i in range(2)]
            ek = [AR.alloc([T], F32, "ek%d" % i) for i in range(2)]
            for (pc, off, dst) in ((0, O_Q, qin), (1, O_K, kin)):
                sl, v = wload([(win_cols(l, off, 512), lambda v: v)], lambda a: a.rearrange("p (k c) -> p k c", k=8))
                for h in range(4):
                    b = bank()
                    for k in range(8):
                        mm(b.ap[:, 0:T], v[:, k, h * 128:(h + 1) * 128], hT[k].ap, k == 0, k == 7, [sl, hT[k]], [b])
                    bg = bank()
                    for s in range(NS):
                        mm(bg.ap[:, s * 128:(s + 1) * 128], gkpos.ap[:, s, h * 128:(h + 1) * 128], C("blktri"), True, True, [gkpos, cc], [bg])
                    if pc == 0:
                        e_ = eq[h % 2]
                        act(e_.ap, bg.ap[:, 0:T], AF.Exp, [bg], [e_])
                        vcopy(egl.ap[:, h, :], e_.ap[:, 63:T:64], [e_], [egl])
                        stt(qin[h].ap, b.ap[:, 0:T], float(128 ** -0.5), e_.ap, ALU.mult, ALU.mult, [b, e_], [qin[h]])
                    else:
                        e_ = ek[h % 2]
                        act(e_.ap, bg.ap[:, 0:T], AF.Exp, [bg], [e_], scale=-1.0)
                        tt(kin[h].ap, b.ap[:, 0:T], e_.ap, ALU.mult, [b, e_], [kin[h]])
            for (off, npc, kind) in ((O_K, 1, "k"), (O_V, 2, "v"), (O_G, 2, "g")):
                for pc in range(npc):
                    sl, v = wload([(win_cols(l, off + pc * 512, 512), lambda v: v)], lambda a: a.rearrange("p (k c) -> p k c", k=8))
                    for s in range(NS):
                        b = bank()
                        for k in range(8):
                            mm(b.ap, hT[k].ap[:, s * 128:(s + 1) * 128], v[:, k, :], k == 0, k == 7, [sl, hT[k]], [b])
                        if kind == "k":
                            vcopy(ktm.ap[:, s, :], b.ap, [b], [ktm])
                        elif kind == "v":
                            acopy(vbf.ap[:, s, pc * 512:(pc + 1) * 512], b.ap, [b], [vbf])
                        else:
                            act(sgt.ap[:, s, pc * 512:(pc + 1) * 512], b.ap, AF.Silu, [b], [sgt])
            P.barrier()
            AR.off = mark
            kend = AR.alloc([512], BF16, "kend")
            ekd = AR.alloc([512], F32, "ekd")
            scm = AR.alloc([512], BF16, "scm")
            ov = AR.alloc([1024], F32, "ov")
            onb = AR.alloc([1024], BF16, "onb")
            ss4 = AR.alloc([4], F32, "ss4g")
            junk = AR.alloc([256], F32, "junkg")
            gnw = PPs(l, "gnw")
            for s in range(NS):
                tsl = slice(s * 128, (s + 1) * 128)
                b = bank()
                mm(b.ap, C("blkgt"), gkpos.ap[:, s, :], True, True, [cc, gkpos], [b])
                act(ekd.ap, b.ap, AF.Exp, [b], [ekd])
                tt(kend.ap, ktm.ap[:, s, :], ekd.ap, ALU.mult, [ktm, ekd], [kend])
                b = bank()
                for h in range(4):
                    mm(b.ap[:, h * 128:(h + 1) * 128], kin[h].ap[:, tsl], qin[h].ap[:, tsl], True, True, [kin[h], qin[h]], [b])
                tt(scm.ap.rearrange("p (h l) -> p h l", h=4), b.ap.rearrange("p (h l) -> p h l", h=4),
                   C("maskg").unsqueeze(1).to_broadcast([128, 4, 128]), ALU.mult, [b, cc], [scm])
                bo = [bank(), bank()]
                for hh in range(2):
                    for h in range(4):
                        bb_ = bo[h // 2]
                        oreg = bb_.ap[hh * 64:(hh + 1) * 64, (h % 2) * 256:(h % 2 + 1) * 256]
                        mm(oreg, scm.ap[:, h * 128 + hh * 64:h * 128 + (hh + 1) * 64], vbf.ap[:, s, h * 256:(h + 1) * 256],
                           True, False, [scm, vbf], [bb_])
                        mm(oreg, qin[h].ap[:, s * 128 + hh * 64:s * 128 + (hh + 1) * 64], Gstb[l].ap[:, h * 256:(h + 1) * 256],
                           False, True, [qin[h], Gstb[l]], [bb_])
                    bs = [bank(), bank()]
                    for h in range(4):
                        bb_ = bs[h // 2]
                        mm(bb_.ap[:, (h % 2) * 256:(h % 2 + 1) * 256], kend.ap[hh * 64:(hh + 1) * 64, h * 128:(h + 1) * 128],
                           vbf.ap[hh * 64:(hh + 1) * 64, s, h * 256:(h + 1) * 256], True, True, [kend, vbf], [bb_])
                    for h in range(4):
                        bb_ = bs[h // 2]
                        ci = 2 * s + hh
                        stt(Gst[l].ap[:, h * 256:(h + 1) * 256], Gst[l].ap[:, h * 256:(h + 1) * 256], egl.ap[:, h, ci:ci + 1],
                            bb_.ap[:, (h % 2) * 256:(h % 2 + 1) * 256], ALU.mult, ALU.add, [Gst[l], egl, bb_], [Gst[l]])
                    acopy(Gstb[l].ap, Gst[l].ap, [Gst[l]], [Gstb[l]])
                for hb in range(2):
                    vcopy(ov.ap[:, hb * 512:(hb + 1) * 512], bo[hb].ap, [bo[hb]], [ov])
                memset(ss4.ap, 0.0, [ss4])
                for h in range(4):
                    act(junk.ap, ov.ap[:, h * 256:(h + 1) * 256], AF.Square, [ov], [junk, ss4], accum=ss4.ap[:, h:h + 1])
                ts(ss4.ap, ss4.ap, 1.0 / 256, EPS, ALU.mult, ALU.add, [ss4], [ss4])
                act(ss4.ap, ss4.ap, AF.Ln, [ss4], [ss4])
                act(ss4.ap, ss4.ap, AF.Exp, [ss4], [ss4], scale=-0.5)
                v4 = lambda a: a.rearrange("p (h q) -> p h q", h=4)
                tt(v4(ov.ap), v4(ov.ap), ss4.ap.unsqueeze(2).to_broadcast([128, 4, 256]), ALU.mult, [ov, ss4], [ov])
                tt(v4(ov.ap), v4(ov.ap), gnw.unsqueeze(1).to_broadcast([128, 4, 256]), ALU.mult, [ov, pp[l]], [ov])
                tt(onb.ap, ov.ap, sgt.ap[:, s, :], ALU.mult, [ov, sgt], [onb])
                for hb in range(2):
                    b = bank()
                    for c4 in range(4):
                        c = hb * 4 + c4
                        mm(b.ap[:, c4 * 128:(c4 + 1) * 128], onb.ap[:, c * 128:(c + 1) * 128], identb.ap, True, True, [onb, identb], [b])
                    for c4 in range(4):
                        c = hb * 4 + c4
                        acopy(oT[c].ap[:, tsl], b.ap[:, c4 * 128:(c4 + 1) * 128], [b], [oT[c]])
            P.barrier()
            AR.off = mark
            outproj_gate(w_og, oT, 2, False)

            P.barrier()
            AR.reset()
            mb = [AR.alloc([T], BF16, "mb%d" % k) for k in range(8)]
            for k in range(8):
                acopy(mb[k].ap, mg[k].ap, [mg[k]], [mb[k]])
            for pc in range(2):
                sl, v = wload([(wsq_cols(w_o, l, pc * 512, 512), lambda v: v)], lambda a: a.rearrange("p (k c) -> p k c", k=8))
                for j4 in range(4):
                    j = pc * 4 + j4
                    b = bank()
                    for k in range(8):
                        mm(b.ap[:, 0:T], v[:, k, j4 * 128:(j4 + 1) * 128], mb[k].ap, k == 0, k == 7, [sl, mb[k]], [b])
                    tt(xT[j].ap, xT[j].ap, b.ap[:, 0:T], ALU.add, [xT[j], b], [xT[j]])

            P.barrier()
            rs = rmsnorm_to_hT(None, None)
            apply_norm(rs, PPs(l, "nw_mlp"), pp[l], hT)
            P.barrier()
            AR.reset()
            hid = [AR.alloc([T], BF16, "hid%d" % f) for f in range(32)]
            rl = [AR.alloc([T], F32, "rl%d" % i) for i in range(2)]
            for pc in range(8):
                sl, v = wload([(wsq_cols(w_up, l, pc * 512, 512), lambda v: v)], lambda a: a.rearrange("p (k c) -> p k c", k=8))
                for f4 in range(4):
                    f = pc * 4 + f4
                    b = bank()
                    for k in range(8):
                        mm(b.ap[:, 0:T], v[:, k, f4 * 128:(f4 + 1) * 128], hT[k].ap, k == 0, k == 7, [sl, hT[k]], [b])
                    r_ = rl[f % 2]
                    act(r_.ap, b.ap[:, 0:T], AF.Relu, [b], [r_])
                    tt(hid[f].ap, r_.ap, r_.ap, ALU.mult, [r_], [hid[f]])
            for j in range(8):
                src = w_dn[l].rearrange("(f p) c -> p f c", p=128)[:, :, j * 128:(j + 1) * 128]
                sl, v = wload([(src, lambda v: v)], lambda a: a.rearrange("p (f c) -> p f c", f=32))
                b = bank()
                for f in range(32):
                    mm(b.ap[:, 0:T], v[:, f, :], hid[f].ap, f == 0, f == 31, [sl, hid[f]], [b])
                tt(xT[j].ap, xT[j].ap, b.ap[:, 0:T], ALU.add, [xT[j], b], [xT[j]])

        P.barrier()
        rs = rmsnorm_to_hT(None, None)
        yf = [AR.alloc([T], F32, "yf%d" % i) for i in range(2)]
        xo = AR.alloc([NS, D], F32, "xo")
        for k in range(8):
            y_ = yf[k % 2]
            stt(y_.ap, xT[k].ap, nwf[:, k:k + 1], rs.ap, ALU.mult, ALU.mult, [xT[k], cc, rs], [y_])
            b = bank()
            for s in range(NS):
                mm(b.ap[:, s * 128:(s + 1) * 128], y_.ap[:, s * 128:(s + 1) * 128], C("ident"), True, True, [y_, cc], [b])
            vcopy(xo.ap[:, :, k * 128:(k + 1) * 128], b.ap[:, 0:T].rearrange("p (s c) -> p s c", s=NS), [b], [xo])
        last_store = P.dma("sp", lambda e, xo=xo, tok0=tok0: e.dma_start(
            out=out_d[tok0:tok0 + T, :].rearrange("(s p) d -> p s d", p=128), in_=xo.ap), [xo], [])
        final_stores.append(last_store)

    P.emit_block(final_stores)
    return nc


def _consts():
    i = np.arange(128)
    ident = (i[:, None] == i[None, :]).astype(np.float32)
    ones = np.ones((128, 128), np.float32)
    tri = (i[:, None] <= i[None, :]).astype(np.float32)
    gt = (i[:, None] > i[None, :]).astype(np.float32)
    same = (i[:, None] // 64) == (i[None, :] // 64)
    blktri = (same & (i[:, None] <= i[None, :])).astype(np.float32)
    blkgt = (same & (i[:, None] > i[None, :])).astype(np.float32)
    return ident, ones, tri, gt, blktri * (-1.0 / 16.0), blkgt * (-1.0 / 16.0), blktri


def _packs(inp, L):
    pk = np.zeros((L, 128, NPAR), np.float32)

    def put(l, name, arr):
        o, w = PP[name]
        pk[l, :, o:o + w] = arr

    for l in range(L):
        put(l, "nw_mix", inp["norm_mix_w"][l].reshape(8, 128).T)
        put(l, "nw_mlp", inp["norm_mlp_w"][l].reshape(8, 128).T)
        put(l, "cwA", inp["conv_a_w"][l].reshape(3, 8, 128).transpose(2, 1, 0).reshape(128, 24))
        put(l, "cwB", inp["ssm_conv_w"][l].reshape(4, 16, 128).transpose(2, 1, 0).reshape(128, 64))
        put(l, "cbB", inp["ssm_conv_b"][l].reshape(16, 128).T)
        put(l, "dtb", np.broadcast_to(inp["ssm_dt_bias"][l][None, :], (128, 16)))
        put(l, "alog", np.broadcast_to(inp["ssm_a_log"][l][None, :], (128, 16)))
        put(l, "dsk", np.broadcast_to(inp["ssm_d"][l][None, :], (128, 16)))
        put(l, "snw", np.broadcast_to(inp["ssm_norm_w"][l][None, :], (128, 1024)))
        put(l, "gnw", np.broadcast_to(inp["gla_norm_w"][l][None, :], (128, 256)))
        o, w = PP["wgk"]
        pk[l, 0:16, o:o + w] = inp["gla_w_gk2"][l]
        pk[l, 16, o:o + w] = inp["gla_b_gk"][l]
    cs = np.zeros((128, NCONST), np.float32)
    for i, c in enumerate(_consts()):
        cs[:, i * 128:(i + 1) * 128] = c
    cs[:, 7 * 128:7 * 128 + 8] = inp["norm_f_w"].reshape(8, 128).T
    return pk, cs


def make_in_maps(inp, n_seq, L):
    pk, cs = _packs(inp, L)
    f = lambda a: np.ascontiguousarray(np.asarray(a, dtype=np.float32))
    shared = {
        "w_in": f(inp["w_in"]), "w_out_a": f(inp["w_out_a"]), "w_out_ssm": f(inp["w_out_ssm"]),
        "w_out_gla": f(inp["w_out_gla"]), "w_o": f(inp["w_o"]), "w_mlp_up": f(inp["w_mlp_up"]),
        "w_mlp_down": f(inp["w_mlp_down"]), "ppack": pk, "cpack": cs,
    }
    maps = []
    for b in range(n_seq):
        m = dict(shared)
        m["x"] = f(inp["x"][b])
        maps.append(m)
    return maps


_NC_CACHE = {}


def kernel(**inputs):
    inp = {k: np.asarray(v) for k, v in inputs.items()}
    B, S, _ = inp["x"].shape
    L = inp["w_in"].shape[0]
    T = 512
    key = (S // T, T, L)
    if key not in _NC_CACHE:
        _NC_CACHE[key] = build(S // T, T, L)
    nc = _NC_CACHE[key]
    maps = make_in_maps(inp, B, L)
    in_maps = [maps[c] for c in range(B)]
    res = run_bass_kernel_spmd(nc, in_maps, core_ids=list(range(B)))
    out = np.stack([np.asarray(res.results[b]["out"], dtype=np.float32) for b in range(B)], axis=0)
    return out
```
